# Optimizing a Trainium2 kernel written in Bass

```python
import jax, jax.numpy as jnp
from jax import lax
import numpy as np

D_MODEL = 1024
BATCH = 4
SEQ = 8192
DEPTH = 4

CHUNK = 64
N_META = 16
Q_BLOCK = 128
EPS = 1e-6
ROPE_THETA = 10000.0
CONV_W = 256
CONV_K = 3
MLA_HEADS = 8
QK_NOPE = 64
QK_ROPE = 32
QK_HEAD = QK_NOPE + QK_ROPE
V_HEAD = 64
MLA_W = MLA_HEADS * V_HEAD
Q_RANK = 256
KV_RANK = 128
RET_HEADS = 4
RET_HEAD = 64
RET_W = RET_HEADS * RET_HEAD
D_MIX = CONV_W + MLA_W + RET_W
IN_SPLITS = (CONV_W, CONV_W, CONV_W, CONV_W, Q_RANK, KV_RANK, QK_ROPE, MLA_W, RET_W, RET_W, RET_W, RET_W)
D_IN = int(np.sum(IN_SPLITS))
IN_OFFSETS = tuple(int(o) for o in np.cumsum(IN_SPLITS)[:-1])

kernel_name = "hybrid_conv_mla_retention_trunk"


def rms_norm(x, w):
    xf = x.astype(jnp.float32)
    y = xf * lax.rsqrt(jnp.mean(xf * xf, axis=-1, keepdims=True) + EPS)
    return (y * w.astype(jnp.float32)).astype(x.dtype)


def rope_tables(n_pos, dim, dtype):
    inv = 1.0 / (ROPE_THETA ** (jnp.arange(0, dim, 2, dtype=jnp.float32) / dim))
    ang = jnp.arange(n_pos, dtype=jnp.float32)[:, None] * inv[None, :]
    return jnp.cos(ang).astype(dtype), jnp.sin(ang).astype(dtype)


def apply_rope(x, cos, sin):
    x1, x2 = jnp.split(x, 2, axis=-1)
    c = cos[:, None, :]
    s = sin[:, None, :]
    return jnp.concatenate([x1 * c - x2 * s, x1 * s + x2 * c], axis=-1)


def chunk_ids(n):
    pos = jnp.arange(n)
    return jnp.where(pos < N_META, 0, 1 + (pos - N_META) // CHUNK)


def short_conv_mixer(xin, b, c, z, conv_w, conv_b):
    L = xin.shape[1]
    u = jnp.pad(c * xin, ((0, 0), (CONV_K - 1, 0), (0, 0)))
    y = conv_b
    for j in range(CONV_K):
        y = y + u[:, j:j + L] * conv_w[j]
    return b * y * jax.nn.silu(z)


def mla_mixer(c_q, c_kv, k_rope, z, q_a_norm, w_uq, kv_a_norm, w_ukv, q_norm, k_norm, cos, sin):
    B, L, _ = c_q.shape
    q = (rms_norm(c_q, q_a_norm) @ w_uq).reshape(B, L, MLA_HEADS, QK_HEAD)
    kv = (rms_norm(c_kv, kv_a_norm) @ w_ukv).reshape(B, L, MLA_HEADS, QK_NOPE + V_HEAD)
    k_nope, v = kv[..., :QK_NOPE], kv[..., QK_NOPE:]
    k_r = jnp.broadcast_to(k_rope[:, :, None, :], (B, L, MLA_HEADS, QK_ROPE))
    k = jnp.concatenate([k_nope, k_r], axis=-1)
    q = rms_norm(q, q_norm)
    k = rms_norm(k, k_norm)
    q = jnp.concatenate([q[..., :QK_NOPE], apply_rope(q[..., QK_NOPE:], cos, sin)], axis=-1)
    k = jnp.concatenate([k[..., :QK_NOPE], apply_rope(k[..., QK_NOPE:], cos, sin)], axis=-1)
    Lp = -(-L // Q_BLOCK) * Q_BLOCK
    pad = ((0, 0), (0, Lp - L), (0, 0), (0, 0))
    q = jnp.pad(q, pad)
    k = jnp.pad(k, pad)
    v = jnp.pad(v, pad)
    cid = chunk_ids(Lp)
    n_blk = Lp // Q_BLOCK
    q_blocks = q.reshape(B, n_blk, Q_BLOCK, MLA_HEADS, QK_HEAD).transpose(1, 0, 2, 3, 4)
    cid_blocks = cid.reshape(n_blk, Q_BLOCK)
    scale = QK_HEAD ** -0.5
    neg = jnp.finfo(jnp.float32).min

    def attend(args):
        qb, qc = args
        s = jnp.einsum('bqhd,bkhd->bhqk', qb, k).astype(jnp.float32) * scale
        mask = cid[None, :] <= qc[:, None]
        s = jnp.where(mask[None, None], s, neg)
        p = jax.nn.softmax(s, axis=-1).astype(v.dtype)
        return jnp.einsum('bhqk,bkhd->bqhd', p, v)

    o = lax.map(attend, (q_blocks, cid_blocks))
    o = o.transpose(1, 0, 2, 3, 4).reshape(B, Lp, MLA_W)[:, :L]
    return o * jax.nn.silu(z)


def retention_mixer(q, k, v, z, ret_norm, cos, sin):
    B, L, _ = q.shape
    dt = q.dtype
    q = apply_rope(q.reshape(B, L, RET_HEADS, RET_HEAD), cos, sin)
    k = apply_rope(k.reshape(B, L, RET_HEADS, RET_HEAD), cos, sin) * (RET_HEAD ** -0.5)
    v = v.reshape(B, L, RET_HEADS, RET_HEAD)
    front = CHUNK - N_META
    back = (-(front + L)) % CHUNK
    pad = ((0, 0), (front, back), (0, 0), (0, 0))
    n_chk = (front + L + back) // CHUNK
    qc = jnp.pad(q, pad).reshape(B, n_chk, CHUNK, RET_HEADS, RET_HEAD)
    kc = jnp.pad(k, pad).reshape(B, n_chk, CHUNK, RET_HEADS, RET_HEAD)
    vc = jnp.pad(v, pad).reshape(B, n_chk, CHUNK, RET_HEADS, RET_HEAD)
    log_g = jnp.log1p(-jnp.exp2(-5.0 - jnp.arange(RET_HEADS, dtype=jnp.float32)))
    idx = jnp.arange(CHUNK, dtype=jnp.float32)
    intra_decay = jnp.exp(log_g[:, None, None] * jnp.abs(idx[:, None] - idx[None, :])).astype(dt)
    kv_decay = jnp.exp(log_g[:, None] * (CHUNK - 1.0 - idx)[None, :]).astype(dt)
    q_decay = jnp.exp(log_g[:, None] * (idx + 1.0)[None, :]).astype(dt)
    chunk_decay = jnp.exp(log_g * CHUNK).astype(dt)
    s = jnp.einsum('bnchd,bnmhd->bnhcm', qc, kc) * intra_decay
    o_intra = jnp.einsum('bnhcm,bnmhd->bnchd', s, vc)
    kv = jnp.einsum('bnmhd,hm,bnmhe->nbhde', kc, kv_decay, vc)

    def step(state, kv_n):
        return state * chunk_decay[None, :, None, None] + kv_n, state

    _, prev = lax.scan(step, jnp.zeros_like(kv[0]), kv)
    o_cross = jnp.einsum('bnchd,nbhde->bnche', qc, prev) * q_decay.T[None, None, :, :, None]
    o = (o_intra + o_cross).reshape(B, n_chk * CHUNK, RET_HEADS, RET_HEAD)[:, front:front + L]
    o = rms_norm(o, ret_norm).reshape(B, L, RET_W)
    return o * jax.nn.silu(z)


def setup_inputs(seed: int = 0) -> dict:
    key = jax.random.key(seed)
    ks = jax.random.split(key, 16)
    f32 = jnp.float32

    def nrm(k, shape, scale):
        return jax.random.normal(k, shape, f32) * scale

    def gain(k, shape):
        return 1.0 + 0.05 * jax.random.normal(k, shape, f32)

    return {
        "x": jax.random.normal(ks[0], (BATCH, SEQ, D_MODEL), f32),
        "meta_tokens": nrm(ks[1], (N_META, D_MODEL), 1.0),
        "ln_w": gain(ks[2], (DEPTH, D_MODEL)),
        "w_in": nrm(ks[3], (DEPTH, D_MODEL, D_IN), D_MODEL ** -0.5),
        "w_out": nrm(ks[4], (DEPTH, D_MIX, D_MODEL), D_MIX ** -0.5),
        "conv_w": nrm(ks[5], (DEPTH, CONV_K, CONV_W), CONV_K ** -0.5),
        "conv_b": nrm(ks[6], (DEPTH, CONV_W), 0.02),
        "q_a_norm": gain(ks[7], (DEPTH, Q_RANK)),
        "w_uq": nrm(ks[8], (DEPTH, Q_RANK, MLA_HEADS * QK_HEAD), Q_RANK ** -0.5),
        "kv_a_norm": gain(ks[9], (DEPTH, KV_RANK)),
        "w_ukv": nrm(ks[10], (DEPTH, KV_RANK, MLA_HEADS * (QK_NOPE + V_HEAD)), KV_RANK ** -0.5),
        "q_norm": gain(ks[11], (DEPTH, QK_HEAD)),
        "k_norm": gain(ks[12], (DEPTH, QK_HEAD)),
        "ret_norm": gain(ks[13], (DEPTH, RET_HEAD)),
    }


def reference(x, meta_tokens, ln_w, w_in, w_out, conv_w, conv_b, q_a_norm, w_uq, kv_a_norm, w_ukv,
              q_norm, k_norm, ret_norm):
    B = x.shape[0]
    meta = jnp.broadcast_to(meta_tokens[None].astype(x.dtype), (B, N_META, D_MODEL))
    h = jnp.concatenate([meta, x], axis=1)
    L = h.shape[1]
    cos_m, sin_m = rope_tables(L, QK_ROPE, h.dtype)
    cos_r, sin_r = rope_tables(L, RET_HEAD, h.dtype)
    for l in range(DEPTH):
        u = rms_norm(h, ln_w[l])
        (cx, cb, cc, cz, cq, ckv, krope, mz, rq, rk, rv, rz) = jnp.split(u @ w_in[l], IN_OFFSETS, axis=-1)
        y_conv = short_conv_mixer(cx, cb, cc, cz, conv_w[l], conv_b[l])
        y_mla = mla_mixer(cq, ckv, krope, mz, q_a_norm[l], w_uq[l], kv_a_norm[l], w_ukv[l],
                          q_norm[l], k_norm[l], cos_m, sin_m)
        y_ret = retention_mixer(rq, rk, rv, rz, ret_norm[l], cos_r, sin_r)
        y = jnp.concatenate([y_conv, y_mla, y_ret], axis=-1)
        h = h + y @ w_out[l]
    return h[:, N_META:]
```

```python
import contextlib
import os
import types
import numpy as np
import concourse.bass as bass
import concourse.mybir as mybir
from concourse.bass_utils import run_bass_kernel_spmd

F32 = mybir.dt.float32
BF16 = mybir.dt.bfloat16
AF = mybir.ActivationFunctionType
ALU = mybir.AluOpType
AX = mybir.AxisListType

D_MODEL = 1024
N_META = 16
EPS = 1e-6
D_IN = 2976
NF = 1536
NTM = 1440
NSP = 8 + 2 + 1 + 6 + 2 + 96 + 96 + 64
SP_LNW, SP_QAN, SP_KVAN, SP_CW, SP_CB, SP_WQ, SP_WK, SP_WR = 0, 8, 10, 11, 17, 19, 115, 211
KCH = 1024

ENGS = ("pe", "act", "dve", "pool", "sp")
_DBG = float(os.environ.get("K_DBG", "99"))
TILES_PER_BLOCK = int(os.environ.get("K_TPB", "4"))


class Inst:
    __slots__ = ("id", "eng", "fn", "deps", "dma", "semkey", "semval", "needed")

    def __init__(self, id, eng, fn, dma):
        self.id = id
        self.eng = eng
        self.fn = fn
        self.deps = []
        self.dma = dma
        self.semkey = None
        self.semval = 0
        self.needed = False


def _freeze(fn):
    if fn.__closure__ is None:
        return fn
    cells = []
    for c in fn.__closure__:
        try:
            cells.append(types.CellType(c.cell_contents))
        except ValueError:
            cells.append(c)
    return types.FunctionType(fn.__code__, fn.__globals__, fn.__name__, fn.__defaults__, tuple(cells))


class Phase:
    _uid = 0

    def __init__(self, nc, sync_same=True):
        self.nc = nc
        self.insts = []
        self.by_eng = {e: [] for e in ENGS}
        self.last_w = {}
        self.reads = {}
        self.sync_same = sync_same
        self.dma_keys = []
        self.dma_cum = {}

    def op(self, eng, fn, reads=(), writes=(), dma=None):
        ins = Inst(len(self.insts), eng, _freeze(fn), dma)
        deps = set()
        for k in reads:
            w = self.last_w.get(k)
            if w is not None:
                deps.add(w)
        for k in writes:
            w = self.last_w.get(k)
            if w is not None:
                deps.add(w)
            for r in self.reads.get(k, ()):
                deps.add(r)
        deps.discard(ins.id)
        ins.deps = [(d, self.dma_cum.get(self.insts[d].dma) if self.insts[d].dma is not None else None)
                    for d in sorted(deps)]
        for k in reads:
            self.reads.setdefault(k, []).append(ins.id)
        for k in writes:
            self.last_w[k] = ins.id
            self.reads[k] = []
        if dma is not None:
            if dma not in self.dma_keys:
                self.dma_keys.append(dma)
            self.dma_cum[dma] = self.dma_cum.get(dma, 0) + 16
        self.insts.append(ins)
        self.by_eng[eng].append(ins)
        return ins

    def emit(self):
        nc = self.nc
        insts = self.insts
        for ins in insts:
            for d, _v in ins.deps:
                di = insts[d]
                if di.dma is not None or di.eng != ins.eng or (self.sync_same and ins.eng != "pe"):
                    di.needed = True
        for ins in insts:
            if ins.dma is not None:
                ins.needed = True
        cnt = {}
        for e in ENGS:
            c = 0
            for ins in self.by_eng[e]:
                if ins.dma is not None:
                    k = ("dma", ins.dma)
                    cnt[k] = cnt.get(k, 0) + 16
                    ins.semkey = k
                    ins.semval = cnt[k]
                elif ins.needed:
                    c += 1
                    ins.semkey = ("eng", e)
                    ins.semval = c
        semkeys = [("eng", e) for e in ENGS] + [("dma", k) for k in self.dma_keys]
        dma_final = {("dma", k): cnt.get(("dma", k), 0) for k in self.dma_keys}
        with contextlib.ExitStack() as st:
            st.enter_context(nc.cleanup_on_exit())
            sems = {}
            for k in semkeys:
                Phase._uid += 1
                sems[k] = nc.alloc_semaphore("s%d_%s_%s" % (Phase._uid, k[0], str(k[1])))
            block = st.enter_context(nc.Block())

            def body_for(e):
                def body(engobj):
                    waited = {}
                    for ins in self.by_eng[e]:
                        for d, dv in ins.deps:
                            di = insts[d]
                            if di.semkey is None:
                                continue
                            if di.dma is None and di.eng == e and (e == "pe" or not self.sync_same):
                                continue
                            val = dv if dv is not None else di.semval
                            if waited.get(di.semkey, 0) >= val:
                                continue
                            engobj.wait_ge(sems[di.semkey], val)
                            waited[di.semkey] = val
                        bi = ins.fn(engobj)
                        if ins.needed:
                            bi.then_inc(sems[ins.semkey], 16 if ins.dma is not None else 1)
                    mine = set(i.semkey for i in self.by_eng[e] if i.dma is not None)
                    for k in mine:
                        if waited.get(k, 0) < dma_final[k]:
                            engobj.wait_ge(sems[k], dma_final[k])
                return body

            block.tensor(body_for("pe"))
            block.scalar(body_for("act"))
            block.vector(body_for("dve"))
            block.gpsimd(body_for("pool"))
            block.sync(body_for("sp"))


def _rope_table(L):
    def tab(dim):
        inv = (1.0 / (np.float32(10000.0) ** (np.arange(0, dim, 2, dtype=np.float32) / np.float32(dim)))).astype(np.float32)
        ang = (np.arange(L, dtype=np.float32)[:, None] * inv[None, :]).astype(np.float32)
        return np.cos(ang).astype(np.float32), np.sin(ang).astype(np.float32)
    cm, sm = tab(32)
    cr, sr = tab(64)
    return np.ascontiguousarray(np.concatenate([cm, sm, cr, sr], axis=1).astype(np.float32))


def _ret_tables():
    H = 4
    log_g = np.log1p(-np.exp2(-5.0 - np.arange(H, dtype=np.float64)))
    idx = np.arange(64, dtype=np.float64)
    intra = np.exp(log_g[:, None, None] * np.abs(idx[:, None] - idx[None, :]))
    kvd = np.exp(log_g[:, None] * (63.0 - idx)[None, :])
    qd = np.exp(log_g[:, None] * (idx + 1.0)[None, :])
    cd = np.exp(log_g * 64.0)
    sc = 64.0 ** -0.5
    D2 = np.zeros((128, H, 128), np.float64)
    for a in range(2):
        D2[a * 64:(a + 1) * 64, :, a * 64:(a + 1) * 64] = np.transpose(intra, (1, 0, 2)) * sc
    QDA = np.zeros((128, H, 128), np.float64)
    QDB = np.zeros((128, H, 128), np.float64)
    QDA[:, :, 0:64] = qd[None, :, :]
    QDB[:, :, 64:128] = qd[None, :, :]
    G = np.zeros((128, 2, 64), np.float64)
    for r in range(128):
        for p in range(2):
            G[r, p, :] = cd[2 * p + r // 64]
    tab = np.zeros((128, RT_N), np.float32)
    tab[:, RT_D2:RT_D2 + 512] = D2.reshape(128, 512)
    tab[:, RT_QDA:RT_QDA + 512] = QDA.reshape(128, 512)
    tab[:, RT_QDB:RT_QDB + 512] = QDB.reshape(128, 512)
    tab[:, RT_G:RT_G + 128] = G.reshape(128, 128)
    tab[0:64, RT_KVDA:RT_KVDA + 4] = kvd.T * sc
    tab[64:128, RT_KVDB:RT_KVDB + 4] = kvd.T * sc
    tab[0:16, RT_KVDM:RT_KVDM + 4] = (kvd.T * sc)[48:64]
    return tab


RT_D2, RT_QDA, RT_QDB, RT_G, RT_KVDA, RT_KVDB, RT_KVDM, RT_N = 0, 512, 1024, 1536, 1664, 1668, 1672, 1676

_OFF = np.cumsum([0, 256, 256, 256, 256, 256, 128, 32, 512, 256, 256, 256, 256])
_PERM = np.concatenate([np.arange(_OFF[0], _OFF[4]), np.arange(_OFF[7], _OFF[8]),
                        np.arange(_OFF[4], _OFF[7]), np.arange(_OFF[8], _OFF[12])])


def build_program(SEQ, DEPTH):
    assert SEQ % 512 == 0
    NT = SEQ // 512
    NB = SEQ // 128
    L = N_META + SEQ
    nc = bass.Bass("TRN2", target_bir_lowering=False)

    def din(name, shape, dt=F32):
        return nc.dram_tensor(name, list(shape), dt, kind="ExternalInput").ap()

    x_d = din("x", [SEQ, D_MODEL])
    meta_d = din("meta", [N_META, D_MODEL])
    win_d = din("w_in", [DEPTH, D_MODEL, D_IN])
    wout_d = din("w_out", [DEPTH, D_MODEL, D_MODEL])
    wuq_d = din("w_uq", [DEPTH, 256, 768])
    wukv_d = din("w_ukv", [DEPTH, 128, 1024])
    sp_d = din("sp", [128, DEPTH, NSP])
    rt_d = din("rt", [L, 96])
    rtab_d = din("rtab", [128, RT_N])
    ident_d = din("ident", [128, 128])
    out_d = nc.dram_tensor("out", [SEQ, D_MODEL], F32, kind="ExternalOutput").ap()
    hmeta_d = nc.dram_tensor("hmeta", [N_META, D_MODEL], F32).ap()
    kc_d = nc.dram_tensor("kc", [8, 96, SEQ], BF16).ap()
    vc_d = nc.dram_tensor("vc", [8, 128, NB, 128], BF16).ap()

    with contextlib.ExitStack() as st:
        def sb(name, shape, dt=F32):
            return st.enter_context(nc.sbuf_tensor("sb_" + name, list(shape), dt))

        def ps(name, shape, dt=F32):
            return st.enter_context(nc.psum_tensor(name, list(shape), dt))

        ident = sb("ident", [128, 128], BF16)
        spt = sb("spt", [128, DEPTH, NSP])
        rtab = sb("rtab", [128, RT_N])
        win = sb("win", [128, 8, D_IN], BF16)
        wout = sb("wout", [128, 8, D_MODEL], BF16)
        wuq = sb("wuq", [128, 2, 768], BF16)
        wukv = sb("wukv", [128, 1024], BF16)
        wq_s = sb("wq_s", [128, 96])
        state = sb("state", [128, 2, 64])
        vconv = sb("vconv", [128, 2, 514])
        kT_meta = sb("kT_meta", [96, 8, 16], BF16)
        v_meta = sb("v_meta", [16, 8, 128], BF16)
        v_cur = sb("v_cur", [128, 4, 8, 128], BF16)
        hbuf = sb("hbuf", [128, 2, D_MODEL])
        u_bf = sb("u_bf", [128, 2, D_MODEL], BF16)
        uT = sb("uT", [128, 8, 512], BF16)
        rt_t = sb("rt_t", [128, 4, 96])
        stats = sb("stats", [128, 4, 8])
        st2 = sb("st2", [128, 24])
        junk = sb("junk", [128, 256], BF16)
        yT = sb("yT", [128, 8, 512], BF16)
        gate = sb("gate", [128, 512])
        cacc = sb("cacc", [128, 512])
        proj_t = sb("proj_t", [128, NTM])
        cn_bf = sb("cn_bf", [128, 384], BF16)
        cT = sb("cT", [128, 3, 128], BF16)
        qw = sb("qw", [128, 8, 96])
        kw = sb("kw", [128, 8, 96])
        sqw = sb("sqw", [128, 768])
        sqk = sb("sqk", [128, 768])
        rtb = sb("rtb", [128, 1024])
        rw = sb("rw", [128, 2, 4, 8, 16])
        qk_bf = sb("qk_bf", [128, 2, 8, 96], BF16)
        QT_cur = sb("QT_cur", [96, 8, 512], BF16)
        kT_cur = sb("kT_cur", [96, 8, 512], BF16)
        rqk = sb("rqk", [128, 8, 64])
        rqk_bf = sb("rqk_bf", [128, 8, 64], BF16)
        kdec = sb("kdec", [128, 2, 4, 64], BF16)
        qm = sb("qm", [128, 4, 128], BF16)
        qdm = sb("qdm", [128, 2, 4, 128], BF16)
        rv_bf = sb("rv_bf", [128, 4, 64], BF16)
        rT = sb("rT", [128, 4, 128], BF16)
        SD = sb("SD", [128, 4, 128], BF16)
        st_bf = sb("st_bf", [128, 2, 2, 64], BF16)
        st_tmp = sb("st_tmp", [128, 2, 64])
        o_sb = sb("o_sb", [128, 4, 64])
        o_sq = sb("o_sq", [128, 4, 64])
        rg = sb("rg", [128, 256])
        yr_bf = sb("yr_bf", [128, 256], BF16)
        NKB = 3
        kbuf = [sb("kbuf%d" % i, [96, KCH], BF16) for i in range(NKB)]
        vbuf = [sb("vbuf%d" % i, [128, KCH // 128, 128], BF16) for i in range(NKB)]
        NPB = 3
        pbuf = [sb("pbuf%d" % i, [128, 512], BF16) for i in range(NPB)]
        recip = sb("recip", [128, 512])
        on_sb = sb("on_sb", [128, 512])
        PS = [ps("ps%d" % i, [128, 512]) for i in range(7)]
        PT = ps("pt", [128, 1024], BF16)
        wstage = hbuf[:].rearrange("p a d -> p (a d)")

        ph = Phase(nc)
        ph.op("pool", lambda e: e.dma_start(out=ident[:], in_=ident_d[:, :]), writes=["ident"], dma="ident")
        ph.op("sp", lambda e: e.dma_start(out=spt[:], in_=sp_d[:, :, :]), writes=["spt"], dma="spt")
        ph.op("sp", lambda e: e.dma_start(out=rtab[:], in_=rtab_d[:, :]), writes=["rtab"], dma="rtab")
        ph.op("pool", lambda e: e.memset(v_cur[:].rearrange("p s h c -> p (s h c)"), 1.0), writes=["v_cur"])
        ph.op("pool", lambda e: e.memset(v_meta[:].rearrange("p h c -> p (h c)"), 1.0), writes=["v_meta"])
        ph.op("pool", lambda e: e.memset(qm[:].rearrange("p h c -> p (h c)"), 0.0), writes=["qm"])
        ph.emit()

        def load_weights(l):
            ph = Phase(nc)
            wv = win_d[l].rearrange("(c p) n -> p c n", p=128)
            HW = D_IN // 2
            for c in range(8):
                for hf in range(2):
                    ph.op("sp", lambda e: e.dma_start(out=wstage[:, 0:HW], in_=wv[:, c, hf * HW:(hf + 1) * HW]), writes=["wstage"], dma="wstage")
                    if hf == 0:
                        ph.op("act", lambda e: e.activation(out=win[:, c, 0:HW], in_=wstage[:, 0:HW], func=AF.Copy,
                                                            scale=spt[:, l, SP_LNW + c:SP_LNW + c + 1]),
                              reads=["wstage", "spt"], writes=["win"])
                    else:
                        ph.op("dve", lambda e: e.tensor_scalar(out=win[:, c, HW:D_IN], in0=wstage[:, 0:HW],
                                                               scalar1=spt[:, l, SP_LNW + c:SP_LNW + c + 1], scalar2=None, op0=ALU.mult),
                              reads=["wstage", "spt"], writes=["win"])
            wo = wout_d[l].rearrange("(c p) n -> p c n", p=128)
            ph.op("pool", lambda e: e.dma_start(out=wout[:], in_=wo[:, :, :]), writes=["wout"], dma="wout")
            wq = wuq_d[l].rearrange("(c p) n -> p c n", p=128)
            for c in range(2):
                ph.op("sp", lambda e: e.dma_start(out=wstage[:, 0:768], in_=wq[:, c, :]), writes=["wstage"], dma="wstage")
                ph.op("dve", lambda e: e.tensor_scalar(out=wuq[:, c, :], in0=wstage[:, 0:768],
                                                       scalar1=spt[:, l, SP_QAN + c:SP_QAN + c + 1], scalar2=None, op0=ALU.mult),
                      reads=["wstage", "spt"], writes=["wuq"])
            ph.op("sp", lambda e: e.dma_start(out=wstage[:, 0:1024], in_=wukv_d[l]), writes=["wstage"], dma="wstage")
            ph.op("dve", lambda e: e.tensor_scalar(out=wukv[:], in0=wstage[:, 0:1024],
                                                   scalar1=spt[:, l, SP_KVAN:SP_KVAN + 1], scalar2=None, op0=ALU.mult),
                  reads=["wstage", "spt"], writes=["wukv"])
            ph.op("dve", lambda e: e.tensor_scalar(out=wq_s[:], in0=spt[:, l, SP_WQ:SP_WQ + 96], scalar1=float(96.0 ** -0.5),
                                                   scalar2=None, op0=ALU.mult), reads=["spt"], writes=["wq_s"])
            ph.op("pool", lambda e: e.memset(state[:].rearrange("p a b -> p (a b)"), 0.0), writes=["state"])
            ph.op("pool", lambda e: e.memset(vconv[:].rearrange("p a b -> p (a b)"), 0.0), writes=["vconv"])
            ph.emit()

        def rstd_ops(ph, src_ap, dst_ap, n, keys_r, keys_w):
            ph.op("act", lambda e: e.activation(out=dst_ap, in_=src_ap, func=AF.Ln, scale=1.0 / n, bias=EPS),
                  reads=keys_r, writes=keys_w)
            ph.op("act", lambda e: e.activation(out=dst_ap, in_=dst_ap, func=AF.Exp, scale=-0.5),
                  reads=keys_w, writes=keys_w)

        def silu_ops(ph, z_ap, z_keys, g_ap, g_key):
            ph.op("act", lambda e: e.activation(out=g_ap, in_=z_ap, func=AF.Exp, scale=-1.0), reads=z_keys, writes=[g_key])
            ph.op("act", lambda e: e.activation(out=g_ap, in_=g_ap, func=AF.Ln, bias=1.0), reads=[g_key], writes=[g_key])
            ph.op("act", lambda e: e.activation(out=g_ap, in_=g_ap, func=AF.Exp, scale=-1.0), reads=[g_key], writes=[g_key])
            ph.op("dve", lambda e: e.tensor_tensor(out=g_ap, in0=z_ap, in1=g_ap, op=ALU.mult), reads=list(z_keys) + [g_key], writes=[g_key])

        def tile_program(l, ti, ph):
            meta = ti < 0
            T = N_META if meta else 512
            subs = [(0, N_META)] if meta else [(s * 128, 128) for s in range(4)]
            pos0 = 0 if meta else N_META + ti * 512
            first_layer = (l == 0)

            def hsrc(si, nr):
                if meta:
                    return (meta_d if first_layer else hmeta_d)[:, :]
                src = x_d if first_layer else out_d
                return src[ti * 512 + si * 128:ti * 512 + si * 128 + nr, :]

            if meta:
                ph.op("sp", lambda e: e.dma_start(out=rt_t[0:T, 0, :], in_=rt_d[0:T, :]), writes=["rt_t"], dma="rt_t")
            else:
                rv = rt_d[pos0:pos0 + 512, :].rearrange("(s p) d -> p s d", p=128)
                ph.op("sp", lambda e: e.dma_start(out=rt_t[:], in_=rv), writes=["rt_t"], dma="rt_t")
            for si, (r0, nr) in enumerate(subs):
                hb = si % 2
                hk = "hbuf%d" % hb
                ph.op("sp", lambda e: e.dma_start(out=hbuf[0:nr, hb, :], in_=hsrc(si, nr)), writes=[hk], dma=hk)
                uk = "u_bf%d" % hb
                ph.op("act", lambda e: e.activation(out=u_bf[0:nr, hb, :], in_=hbuf[0:nr, hb, :], func=AF.Square, accum_out=stats[0:nr, si, 0:1]),
                      reads=[hk], writes=[uk, "stats"])
                rstd_ops(ph, stats[0:nr, si, 0:1], stats[0:nr, si, 1:2], float(D_MODEL), ["stats"], ["stats"])
                ph.op("act", lambda e: e.activation(out=u_bf[0:nr, hb, :], in_=hbuf[0:nr, hb, :], func=AF.Copy, scale=stats[0:nr, si, 1:2]),
                      reads=[hk, "stats"], writes=[uk])
                for c in range(8):
                    ph.op("pe", lambda e: e.transpose(out=PT[:, c * 128:c * 128 + nr], in_=u_bf[0:nr, hb, c * 128:(c + 1) * 128], identity=ident[0:nr, 0:nr]),
                          reads=[uk, "ident"], writes=["PT"])
                ph.op("dve", lambda e: e.tensor_copy(out=uT[:, :, r0:r0 + nr], in_=PT[:, :].rearrange("p (c t) -> p c t", c=8)[:, :, 0:nr]),
                      reads=["PT"], writes=["uT"])

            if _DBG <= 1:
                return

            def fmajor(ft, bank):
                for c in range(8):
                    ph.op("pe", lambda e: e.matmul(PS[bank][:, 0:T], lhsT=win[:, c, ft * 128:(ft + 1) * 128], rhs=uT[:, c, 0:T],
                                                   start=(c == 0), stop=(c == 7)),
                          reads=["win", "uT"], writes=["PS%d" % bank])

            def conv_gen(a):
                cwb = SP_CW + a * 3
                fmajor(0 + a, 1)
                yield
                ph.op("act", lambda e: e.activation(out=cacc[:, 0:T], in_=PS[1][:, 0:T], func=AF.Copy), reads=["PS1"], writes=["cacc"])
                yield
                fmajor(4 + a, 1)
                yield
                ph.op("dve", lambda e: e.tensor_tensor(out=vconv[:, a, 2:2 + T], in0=PS[1][:, 0:T], in1=cacc[:, 0:T], op=ALU.mult),
                      reads=["PS1", "cacc"], writes=["vconv"])
                yield
                fmajor(2 + a, 1)
                ph.op("dve", lambda e: e.tensor_scalar(out=cacc[:, 0:T], in0=vconv[:, a, 0:T], scalar1=spt[:, l, cwb:cwb + 1],
                                                       scalar2=spt[:, l, SP_CB + a:SP_CB + a + 1], op0=ALU.mult, op1=ALU.add),
                      reads=["vconv", "spt"], writes=["cacc"])
                yield
                for j in (1, 2):
                    ph.op("dve", lambda e: e.scalar_tensor_tensor(out=cacc[:, 0:T], in0=vconv[:, a, j:j + T], scalar=spt[:, l, cwb + j:cwb + j + 1],
                                                                  in1=cacc[:, 0:T], op0=ALU.mult, op1=ALU.add),
                          reads=["vconv", "spt", "cacc"], writes=["cacc"])
                    yield
                ph.op("pool", lambda e: e.tensor_copy(out=vconv[:, a, 0:2], in_=vconv[:, a, T:T + 2]), reads=["vconv"], writes=["vconv"])
                ph.op("dve", lambda e: e.tensor_tensor(out=cacc[:, 0:T], in0=PS[1][:, 0:T], in1=cacc[:, 0:T], op=ALU.mult),
                      reads=["PS1", "cacc"], writes=["cacc"])
                yield
                fmajor(6 + a, 1)
                yield
                silu_ops(ph, PS[1][:, 0:T], ["PS1"], gate[:, 0:T], "gate")
                yield
                ph.op("pool", lambda e: e.tensor_tensor(out=yT[:, a, 0:T], in0=cacc[:, 0:T], in1=gate[:, 0:T], op=ALU.mult),
                      reads=["cacc", "gate"], writes=["yT%d" % a])
                yield

            if _DBG <= 2:
                return
            def proj_gen(sj):
                rj, nj = subs[sj]
                groups = [(0, 512, 2), (512, 512, 3), (1024, NTM - 1024, 4)]
                for (c0, cn, bank) in groups:
                    for c in range(8):
                        ph.op("pe", lambda e: e.matmul(PS[bank][0:nj, 0:cn], lhsT=uT[:, c, rj:rj + nj], rhs=win[:, c, NF + c0:NF + c0 + cn],
                                                       start=(c == 0), stop=(c == 7)),
                              reads=["win", "uT"], writes=["PS%d" % bank])
                        if c % 4 == 3:
                            yield
                ph.op("act", lambda e: e.activation(out=proj_t[0:nj, 0:512], in_=PS[2][0:nj, 0:512], func=AF.Copy), reads=["PS2"], writes=["proj_a"])
                yield
                ph.op("dve", lambda e: e.tensor_copy(out=proj_t[0:nj, 512:1024], in_=PS[3][0:nj, 0:512]), reads=["PS3"], writes=["proj_b"])
                yield
                ph.op("act", lambda e: e.activation(out=proj_t[0:nj, 1024:NTM], in_=PS[4][0:nj, 0:NTM - 1024], func=AF.Copy), reads=["PS4"], writes=["proj_c"])
                yield

            for si, (r0, nr) in enumerate(subs):
                if si == 0:
                    for _ in proj_gen(0):
                        pass
                if _DBG <= 2.1:
                    continue
                qk = proj_t[0:nr, 416:928].rearrange("p (h d) -> p h d", h=8)
                cosr = rt_t[0:nr, si, 32:64].unsqueeze(1).to_broadcast([nr, 8, 32])
                sinr = rt_t[0:nr, si, 64:96].unsqueeze(1).to_broadcast([nr, 8, 32])
                rtmp = rtb[0:nr, :].rearrange("p (a h d) -> p a h d", a=4, h=8)
                pk = ["proj_a", "proj_b", "rt_t"]
                if _DBG > 3:
                    ph.op("pool", lambda e: e.tensor_tensor(out=rtmp[:, 0], in0=qk[:, :, 0:32], in1=cosr, op=ALU.mult), reads=pk, writes=["rt0"])
                    ph.op("dve", lambda e: e.tensor_tensor(out=rtmp[:, 2], in0=qk[:, :, 0:32], in1=sinr, op=ALU.mult), reads=pk, writes=["rt2"])
                    ph.op("pool", lambda e: e.tensor_tensor(out=rtmp[:, 1], in0=qk[:, :, 32:64], in1=sinr, op=ALU.mult), reads=pk, writes=["rt1"])
                    ph.op("dve", lambda e: e.tensor_tensor(out=rtmp[:, 3], in0=qk[:, :, 32:64], in1=cosr, op=ALU.mult), reads=pk, writes=["rt3"])
                    ph.op("act", lambda e: e.activation(out=rv_bf[0:nr].rearrange("p h d -> p (h d)"), in_=proj_t[0:nr, 928:1184], func=AF.Copy),
                          reads=["proj_b", "proj_c"], writes=["rv_bf"])
                    silu_ops(ph, proj_t[0:nr, 1184:1440], ["proj_c"], rg[0:nr, :], "rg")

                def ret_chain():
                    ph.op("pool", lambda e: e.tensor_tensor(out=rqk[0:nr, :, 0:32], in0=rtmp[:, 0], in1=rtmp[:, 1], op=ALU.subtract), reads=["rt0", "rt1"], writes=["rqk"])
                    ph.op("dve", lambda e: e.tensor_tensor(out=rqk[0:nr, :, 32:64], in0=rtmp[:, 2], in1=rtmp[:, 3], op=ALU.add), reads=["rt2", "rt3", "rqk"], writes=["rqk"])
                    yield
                    ph.op("act", lambda e: e.activation(out=rqk_bf[0:nr].rearrange("p h d -> p (h d)"), in_=rqk[0:nr].rearrange("p h d -> p (h d)"), func=AF.Copy),
                          reads=["rqk"], writes=["rqk_bf"])
                    yield
                    for ch in range(1 if meta else 2):
                        kcol = RT_KVDM if meta else (RT_KVDA if ch == 0 else RT_KVDB)
                        ph.op("pool", lambda e: e.tensor_tensor(out=kdec[0:nr, ch], in0=rqk[0:nr, 4:8, :],
                                                                in1=rtab[0:nr, kcol:kcol + 4].unsqueeze(2).to_broadcast([nr, 4, 64]), op=ALU.mult),
                              reads=["rqk", "rtab"], writes=["kdec%d" % ch])
                        yield
                    for j in range(4):
                        ph.op("pe", lambda e: e.transpose(out=PT[:, j * 128:j * 128 + nr], in_=rqk_bf[0:nr, 2 * j:2 * j + 2, :].rearrange("p a b -> p (a b)"),
                                                          identity=ident[0:nr, 0:nr]),
                              reads=["rqk_bf", "ident"], writes=["PT"])
                    ph.op("dve", lambda e: e.tensor_copy(out=rT[:, :, 0:nr], in_=PT[:, 0:512].rearrange("p (j t) -> p j t", j=4)[:, :, 0:nr]), reads=["PT"], writes=["rT"])
                    yield
                    ph.op("pool", lambda e: e.tensor_copy(out=qm[0:64, 0:4:2, 0:nr], in_=rT[0:64, 0:2, 0:nr]), reads=["rT"], writes=["qm"])
                    ph.op("pool", lambda e: e.tensor_copy(out=qm[64:128, 1:4:2, 0:nr], in_=rT[64:128, 0:2, 0:nr]), reads=["rT", "qm"], writes=["qm"])
                    yield "HEAD_DONE"
                    for h in range(4):
                        ph.op("pe", lambda e: e.matmul(PS[5][0:nr, h * 128:h * 128 + nr], lhsT=rT[:, 2 + h // 2, 0:nr], rhs=qm[:, h, 0:nr], start=True, stop=True),
                              reads=["rT", "qm"], writes=["PS5"])
                    ph.op("dve", lambda e: e.tensor_tensor(out=SD[0:nr, :, 0:nr], in0=PS[5][0:nr, :].rearrange("p (h c) -> p h c", h=4)[:, :, 0:nr],
                                                           in1=rtab[0:nr, RT_D2:RT_D2 + 512].rearrange("p (h c) -> p h c", h=4)[:, :, 0:nr], op=ALU.mult),
                          reads=["PS5", "rtab"], writes=["SD"])
                    yield
                    if not meta:
                        for ch in range(2):
                            qcol = RT_QDA if ch == 0 else RT_QDB
                            ph.op("pool", lambda e: e.tensor_tensor(out=qdm[:, ch], in0=qm[:], in1=rtab[:, qcol:qcol + 512].rearrange("p (h c) -> p h c", h=4), op=ALU.mult),
                                  reads=["qm", "rtab"], writes=["qdm%d" % ch])
                            yield
                        ph.op("pool", lambda e: e.tensor_copy(out=st_bf[:, 0], in_=state[:]), reads=["state"], writes=["st_bf0"])
                        yield
                    nch = 1 if meta else 2
                    for ch in range(nch):
                        for pr in range(2):
                            ph.op("pe", lambda e: e.matmul(PS[6][:, (ch * 2 + pr) * 128:(ch * 2 + pr + 1) * 128],
                                                           lhsT=kdec[0:nr, ch, 2 * pr:2 * pr + 2, :].rearrange("p a b -> p (a b)"),
                                                           rhs=rv_bf[0:nr, 2 * pr:2 * pr + 2, :].rearrange("p a b -> p (a b)"), start=True, stop=True),
                                  reads=["kdec%d" % ch, "rv_bf"], writes=["PS6"])
                        ph.op("pool", lambda e: e.tensor_tensor(out=st_tmp[:], in0=state[:], in1=rtab[:, RT_G:RT_G + 128].rearrange("p (a c) -> p a c", a=2), op=ALU.mult),
                              reads=["state", "rtab"], writes=["st_tmp"])
                        yield
                        kvv6 = PS[6][:, ch * 256:ch * 256 + 256].rearrange("p (a c) -> p a c", a=2)
                        ph.op("dve", lambda e: e.tensor_tensor(out=state[0:64], in0=kvv6[0:64, :, 0:64], in1=st_tmp[0:64], op=ALU.add),
                              reads=["PS6", "st_tmp"], writes=["state"])
                        ph.op("dve", lambda e: e.tensor_tensor(out=state[64:128], in0=kvv6[64:128, :, 64:128], in1=st_tmp[64:128], op=ALU.add),
                              reads=["PS6", "st_tmp", "state"], writes=["state"])
                        yield
                        if ch == 0 and not meta:
                            ph.op("pool", lambda e: e.tensor_copy(out=st_bf[:, 1], in_=state[:]), reads=["state"], writes=["st_bf1"])
                            yield
                    for h in range(4):
                        ph.op("pe", lambda e: e.matmul(PS[0][0:nr, h * 64:(h + 1) * 64], lhsT=SD[0:nr, h, 0:nr], rhs=rv_bf[0:nr, h, :], start=True, stop=meta),
                              reads=["SD", "rv_bf"], writes=["PS0"])
                        if not meta:
                            for ch in range(2):
                                ph.op("pe", lambda e: e.matmul(PS[0][:, h * 64:(h + 1) * 64], lhsT=qdm[:, ch, h, :], rhs=st_bf[:, ch, h // 2, :],
                                                               start=False, stop=(ch == 1)),
                                      reads=["qdm0", "qdm1", "st_bf0", "st_bf1"], writes=["PS0"])
                    rs = st2[0:nr, 16:20]
                    ph.op("act", lambda e: e.activation(out=o_sb[0:nr].rearrange("p h d -> p (h d)"), in_=PS[0][0:nr, 0:256], func=AF.Copy), reads=["PS0"], writes=["o_sb"])
                    yield
                    ph.op("act", lambda e: e.activation(out=o_sq[0:nr].rearrange("p h d -> p (h d)"), in_=o_sb[0:nr].rearrange("p h d -> p (h d)"), func=AF.Square),
                          reads=["o_sb"], writes=["o_sq"])
                    yield
                    ph.op("dve", lambda e: e.tensor_reduce(out=rs, in_=o_sq[0:nr], axis=AX.X, op=ALU.add), reads=["o_sq"], writes=["st2_r"])
                    yield
                    rstd_ops(ph, rs, rs, 64.0, ["st2_r"], ["st2_r"])
                    yield
                    ph.op("dve", lambda e: e.tensor_tensor(out=o_sb[0:nr], in0=o_sb[0:nr], in1=rs.unsqueeze(2).to_broadcast([nr, 4, 64]), op=ALU.mult),
                          reads=["o_sb", "st2_r"], writes=["o_sb"])
                    yield
                    ph.op("pool", lambda e: e.tensor_tensor(out=o_sb[0:nr], in0=o_sb[0:nr], in1=spt[0:nr, l, SP_WR:SP_WR + 64].unsqueeze(1).to_broadcast([nr, 4, 64]), op=ALU.mult),
                          reads=["o_sb", "spt"], writes=["o_sb"])
                    yield
                    ph.op("pool", lambda e: e.tensor_tensor(out=yr_bf[0:nr, :], in0=o_sb[0:nr].rearrange("p h d -> p (h d)"), in1=rg[0:nr, :], op=ALU.mult),
                          reads=["o_sb", "rg"], writes=["yr_bf"])
                    yield
                    for j in range(2):
                        ph.op("pe", lambda e: e.transpose(out=PT[:, j * 128:j * 128 + nr], in_=yr_bf[0:nr, j * 128:(j + 1) * 128], identity=ident[0:nr, 0:nr]),
                              reads=["yr_bf", "ident"], writes=["PT"])
                    ph.op("dve", lambda e: e.tensor_copy(out=yT[:, 6:8, r0:r0 + nr], in_=PT[:, 0:256].rearrange("p (j t) -> p j t", j=2)[:, :, 0:nr]), reads=["PT"], writes=["yT_r"])
                    yield

                rc = ret_chain() if _DBG > 3 else None
                if rc is not None:
                    while next(rc) != "HEAD_DONE":
                        pass
                ph.op("act", lambda e: e.activation(out=junk[0:nr, 0:256], in_=proj_t[0:nr, 0:256], func=AF.Square, accum_out=stats[0:nr, si, 2:3]),
                      reads=["proj_a"], writes=["junk", "stats"])
                ph.op("act", lambda e: e.activation(out=junk[0:nr, 0:128], in_=proj_t[0:nr, 256:384], func=AF.Square, accum_out=stats[0:nr, si, 3:4]),
                      reads=["proj_a"], writes=["junk", "stats"])
                rstd_ops(ph, stats[0:nr, si, 2:3], stats[0:nr, si, 2:3], 256.0, ["stats"], ["stats"])
                rstd_ops(ph, stats[0:nr, si, 3:4], stats[0:nr, si, 3:4], 128.0, ["stats"], ["stats"])
                ph.op("dve", lambda e: e.tensor_scalar(out=cn_bf[0:nr, 0:256], in0=proj_t[0:nr, 0:256], scalar1=stats[0:nr, si, 2:3], scalar2=None, op0=ALU.mult),
                      reads=["proj_a", "stats"], writes=["cn_bf"])
                ph.op("dve", lambda e: e.tensor_scalar(out=cn_bf[0:nr, 256:384], in0=proj_t[0:nr, 256:384], scalar1=stats[0:nr, si, 3:4], scalar2=None, op0=ALU.mult),
                      reads=["proj_a", "stats"], writes=["cn_bf"])
                for j in range(3):
                    ph.op("pe", lambda e: e.transpose(out=PT[:, j * 128:j * 128 + nr], in_=cn_bf[0:nr, j * 128:(j + 1) * 128], identity=ident[0:nr, 0:nr]),
                          reads=["cn_bf", "ident"], writes=["PT"])
                ph.op("dve", lambda e: e.tensor_copy(out=cT[:, :, 0:nr], in_=PT[:, 0:384].rearrange("p (j t) -> p j t", j=3)[:, :, 0:nr]),
                      reads=["PT"], writes=["cT"])
                for half in range(2):
                    for c in range(2):
                        ph.op("pe", lambda e: e.matmul(PS[half][0:nr, 0:384], lhsT=cT[:, c, 0:nr], rhs=wuq[:, c, half * 384:(half + 1) * 384],
                                                       start=(c == 0), stop=(c == 1)),
                              reads=["cT", "wuq"], writes=["PS%d" % half])
                for half in range(2):
                    ph.op("pe", lambda e: e.matmul(PS[2 + half][0:nr, 0:512], lhsT=cT[:, 2, 0:nr], rhs=wukv[:, half * 512:(half + 1) * 512],
                                                   start=True, stop=True),
                          reads=["cT", "wukv"], writes=["PS%d" % (2 + half)])
                if _DBG <= 2.2:
                    continue
                ph.op("act", lambda e: e.activation(out=qw[0:nr, 0:4, :], in_=PS[0][0:nr, 0:384].rearrange("p (h d) -> p h d", h=4), func=AF.Copy),
                      reads=["PS0"], writes=["qw"])
                ph.op("dve", lambda e: e.tensor_copy(out=qw[0:nr, 4:8, :], in_=PS[1][0:nr, 0:384].rearrange("p (h d) -> p h d", h=4)),
                      reads=["PS1"], writes=["qw"])
                if _DBG <= 2.21:
                    continue
                for half in range(2):
                    kvv = PS[2 + half][0:nr, 0:512].rearrange("p (h c) -> p h c", h=4)
                    if half == 0:
                        ph.op("dve", lambda e: e.tensor_copy(out=kw[0:nr, 0:4, 0:64], in_=kvv[:, :, 0:64]), reads=["PS2"], writes=["kw"])
                    else:
                        ph.op("dve", lambda e: e.tensor_copy(out=kw[0:nr, 4:8, 0:64], in_=kvv[:, :, 0:64]), reads=["PS3"], writes=["kw"])
                    if _DBG <= 2.22:
                        continue
                    vdst = (v_meta[0:nr, :, :] if meta else v_cur[0:nr, si, :, :])
                    for hh in range(4):
                        hd = half * 4 + hh
                        cdst = 0 if hd % 2 == 0 else 64
                        ph.op("dve" if hh % 2 == 0 else "act",
                              (lambda e: e.tensor_copy(out=vdst[:, hd, cdst:cdst + 64], in_=kvv[:, hh, 64:128])) if hh % 2 == 0 else
                              (lambda e: e.activation(out=vdst[:, hd, cdst:cdst + 64], in_=kvv[:, hh, 64:128], func=AF.Copy)),
                              reads=["PS%d" % (2 + half)], writes=["vnew"])
                ph.op("dve", lambda e: e.tensor_copy(out=kw[0:nr, :, 64:96], in_=proj_t[0:nr, 384:416].unsqueeze(1).to_broadcast([nr, 8, 32])),
                      reads=["proj_a"], writes=["kw"])
                if _DBG <= 2.3:
                    continue
                def qk_chain(which):
                    wt = qw if which == 0 else kw
                    gain_ap = wq_s[:, :] if which == 0 else spt[:, l, SP_WK:SP_WK + 96]
                    wk = "qw" if which == 0 else "kw"
                    sk = "st2_%d" % which
                    sqk_ = "sqw%d" % which
                    sap = st2[0:nr, which * 8:which * 8 + 8]
                    sqb = sqw if which == 0 else sqk
                    sq3 = sqb[0:nr, 0:768].rearrange("p (h d) -> p h d", h=8)
                    ph.op("act", lambda e: e.activation(out=sqb[0:nr, 0:768], in_=wt[0:nr].rearrange("p h d -> p (h d)"), func=AF.Square), reads=[wk], writes=[sqk_])
                    yield
                    ph.op("dve", lambda e: e.tensor_reduce(out=sap, in_=sq3, axis=AX.X, op=ALU.add), reads=[sqk_], writes=[sk])
                    yield
                    rstd_ops(ph, sap, sap, 96.0, [sk], [sk])
                    yield
                    ph.op("dve", lambda e: e.tensor_tensor(out=wt[0:nr], in0=wt[0:nr], in1=sap.unsqueeze(2).to_broadcast([nr, 8, 96]), op=ALU.mult),
                          reads=[wk, sk], writes=[wk])
                    yield
                    ph.op("pool", lambda e: e.tensor_tensor(out=wt[0:nr], in0=wt[0:nr], in1=gain_ap[0:nr].unsqueeze(1).to_broadcast([nr, 8, 96]), op=ALU.mult),
                          reads=[wk, "wq_s", "spt"], writes=[wk])
                    yield
                    cosb = rt_t[0:nr, si, 0:16].unsqueeze(1).to_broadcast([nr, 8, 16])
                    sinb = rt_t[0:nr, si, 16:32].unsqueeze(1).to_broadcast([nr, 8, 16])
                    x1 = wt[0:nr, :, 64:80]
                    x2 = wt[0:nr, :, 80:96]
                    rk_ = "rw%d_" % which
                    ph.op("pool", lambda e: e.tensor_tensor(out=rw[0:nr, which, 0], in0=x1, in1=cosb, op=ALU.mult), reads=[wk, "rt_t"], writes=[rk_ + "0"])
                    ph.op("dve", lambda e: e.tensor_tensor(out=rw[0:nr, which, 2], in0=x1, in1=sinb, op=ALU.mult), reads=[wk, "rt_t"], writes=[rk_ + "2"])
                    yield
                    ph.op("pool", lambda e: e.tensor_tensor(out=rw[0:nr, which, 1], in0=x2, in1=sinb, op=ALU.mult), reads=[wk, "rt_t"], writes=[rk_ + "1"])
                    ph.op("dve", lambda e: e.tensor_tensor(out=rw[0:nr, which, 3], in0=x2, in1=cosb, op=ALU.mult), reads=[wk, "rt_t"], writes=[rk_ + "3"])
                    yield
                    ph.op("pool", lambda e: e.tensor_tensor(out=x1, in0=rw[0:nr, which, 0], in1=rw[0:nr, which, 1], op=ALU.subtract),
                          reads=[rk_ + "0", rk_ + "1", wk], writes=[wk])
                    ph.op("dve", lambda e: e.tensor_tensor(out=x2, in0=rw[0:nr, which, 2], in1=rw[0:nr, which, 3], op=ALU.add),
                          reads=[rk_ + "2", rk_ + "3", wk], writes=[wk])
                    yield
                    ph.op("act", lambda e: e.activation(out=qk_bf[0:nr, which].rearrange("p h d -> p (h d)"), in_=wt[0:nr].rearrange("p h d -> p (h d)"), func=AF.Copy),
                          reads=[wk], writes=["qk_bf%d" % which])
                    yield
                    if _DBG <= 2.4:
                        return
                    for h in range(8):
                        ph.op("pe", lambda e: e.transpose(out=PT[0:96, h * 128:h * 128 + nr], in_=qk_bf[0:nr, which, h, :], identity=ident[0:nr, 0:nr]),
                              reads=["qk_bf%d" % which, "ident"], writes=["PT"])
                    if meta and which == 1:
                        dstT, dkey = kT_meta[:, :, 0:nr], "kT_meta"
                    else:
                        dstT, dkey = (QT_cur if which == 0 else kT_cur)[:, :, r0:r0 + nr], ("QT_cur" if which == 0 else "kT_cur")
                    srcT = PT[0:96, :].rearrange("p (h t) -> p h t", h=8)[:, :, 0:nr]
                    ph.op("dve", lambda e: e.tensor_copy(out=dstT, in_=srcT), reads=["PT"], writes=[dkey])
                    yield

                gens = [qk_chain(0), qk_chain(1)] + ([rc] if rc is not None else [])
                if si + 1 < len(subs):
                    gens.append(proj_gen(si + 1))
                if meta:
                    def conv_both():
                        yield from conv_gen(0)
                        yield from conv_gen(1)
                    gens.append(conv_both())
                elif si < 2:
                    gens.append(conv_gen(si))
                while gens:
                    for g in list(gens):
                        try:
                            next(g)
                        except StopIteration:
                            gens.remove(g)
            if _DBG <= 4:
                return
            if not meta:
                ph.op("sp", lambda e: e.dma_start(out=kc_d[:, :, ti * 512:(ti + 1) * 512].rearrange("h d t -> d h t"), in_=kT_cur[:]),
                      reads=["kT_cur"], writes=["kc_dram"], dma="kc_st")
                for h in range(8):
                    ph.op("sp", lambda e: e.dma_start(out=vc_d[h, :, ti * 4:(ti + 1) * 4, :], in_=v_cur[:, :, h, :]),
                          reads=["vnew"], writes=["vc_dram"], dma="vc_st")
            descs = []
            loads = []
            nload = 0
            for h in range(8):
                ob = 5 + (h % 2)
                hd = []
                if meta:
                    hd.append(dict(kT=kT_meta[:, h, 0:T], kk=["kT_meta"], v=v_meta[0:T, h, :], vk=["vnew"], nk=T, q0=0, diag=False))
                else:
                    hd.append(dict(kT=kT_meta[:, h, :], kk=["kT_meta"], v=v_meta[:, h, :], vk=["v_meta"], nk=N_META, q0=0, diag=False))
                    npast = ti * 512
                    for k0 in range(0, npast, KCH):
                        kn = min(KCH, npast - k0)
                        bi = nload % NKB
                        nload += 1

                        def ld(h=h, bi=bi, k0=k0, kn=kn):
                            ph.op("sp", lambda e: e.dma_start(out=kbuf[bi][:, 0:kn], in_=kc_d[h, :, k0:k0 + kn]),
                                  reads=["kc_dram"], writes=["kbuf%d" % bi], dma="kbuf%d" % bi)
                            ph.op("sp", lambda e: e.dma_start(out=vbuf[bi][:, 0:kn // 128, :], in_=vc_d[h, :, k0 // 128:(k0 + kn) // 128, :]),
                                  reads=["vc_dram"], writes=["vbuf%d" % bi], dma="vbuf%d" % bi)
                        loads.append((len(descs) + len(hd), ld))
                        for j in range(kn // 128):
                            hd.append(dict(kT=kbuf[bi][:, j * 128:(j + 1) * 128], kk=["kbuf%d" % bi], v=vbuf[bi][:, j, :], vk=["vbuf%d" % bi],
                                           nk=128, q0=0, diag=False))
                    for r in range(4):
                        hd.append(dict(kT=kT_cur[:, h, r * 128:(r + 1) * 128], kk=["kT_cur"], v=v_cur[:, r, h, :], vk=["vnew"], nk=128, q0=r * 128, diag=True))
                for i, d in enumerate(hd):
                    d.update(h=h, ob=ob, first=(i == 0), last=(i == len(hd) - 1))
                descs += hd

            SBK = (0, 1, 2, 4)

            def emit_qk(i):
                d = descs[i]
                h, sbk, nq = d["h"], SBK[i % 4], T - d["q0"]
                kT_ap, nk, q0 = d["kT"], d["nk"], d["q0"]
                ph.op("pe", lambda e: e.matmul(PS[sbk][0:nk, 0:nq], lhsT=kT_ap, rhs=QT_cur[:, h, q0:T], start=True, stop=True),
                      reads=list(d["kk"]) + ["QT_cur"], writes=["PS%d" % sbk])

            def emit_pv(i):
                d = descs[i]
                h, ob, sbk, pb, nq = d["h"], d["ob"], SBK[i % 4], i % NPB, T - d["q0"]
                okey = "PS%d" % ob
                nk, q0, v_ap = d["nk"], d["q0"], d["v"]
                st_flag, last = d["first"], d["last"]
                ph.op("act", lambda e: e.activation(out=pbuf[pb][0:nk, 0:nq], in_=PS[sbk][0:nk, 0:nq], func=AF.Exp), reads=["PS%d" % sbk], writes=["pbuf%d" % pb])
                if d["diag"]:
                    ph.op("pool", lambda e: e.memset(pbuf[pb][64:128, 0:64], 0.0), reads=["pbuf%d" % pb], writes=["pbuf%d" % pb])
                ph.op("pe", lambda e: e.matmul(PS[ob][:, q0:T], lhsT=v_ap, rhs=pbuf[pb][0:nk, 0:nq], start=st_flag, stop=last),
                      reads=list(d["vk"]) + ["pbuf%d" % pb], writes=[okey])
                if last:
                    orow = slice(0, 64) if h % 2 == 0 else slice(64, 128)
                    srow = slice(64, 128) if h % 2 == 0 else slice(0, 64)
                    if ti < 4:
                        ph.op("act", lambda e: e.activation(out=recip[srow, 0:T], in_=PS[ob][srow, 0:T], func=AF.Ln), reads=[okey], writes=["recip%d" % (h % 2)])
                        ph.op("act", lambda e: e.activation(out=recip[srow, 0:T], in_=recip[srow, 0:T], func=AF.Exp, scale=-1.0),
                              reads=["recip%d" % (h % 2)], writes=["recip%d" % (h % 2)])
                    else:
                        ph.op("dve", lambda e: e.reciprocal(out=recip[srow, 0:T], in_=PS[ob][srow, 0:T]), reads=[okey], writes=["recip%d" % (h % 2)])
                    ph.op("dve", lambda e: e.tensor_tensor(out=on_sb[orow, 0:T], in0=PS[ob][orow, 0:T], in1=recip[srow, 0:T], op=ALU.mult),
                          reads=[okey, "recip%d" % (h % 2)], writes=["on_sb%d" % (h % 2)])
                    if h % 2 == 1:
                        ph.op("pool", lambda e: e.tensor_tensor(out=yT[:, 2 + h // 2, 0:T], in0=on_sb[:, 0:T], in1=gate[:, 0:T], op=ALU.mult),
                              reads=["on_sb0", "on_sb1", "gate"], writes=["yT_m%d" % (h // 2)])
                        if h < 7:
                            emit_gate(h // 2 + 1)

            def emit_gate(a):
                fmajor(8 + a, 3)
                silu_ops(ph, PS[3][:, 0:T], ["PS3"], gate[:, 0:T], "gate")

            emit_gate(0)
            LA, LL = 3, 12
            nd = len(descs)
            li = 0
            for i in range(nd + LA):
                while li < len(loads) and loads[li][0] <= i + LL:
                    loads[li][1]()
                    li += 1
                if i < nd:
                    emit_qk(i)
                if i - LA >= 0:
                    emit_pv(i - LA)
            if _DBG <= 5:
                return
            ykeys = ["yT0", "yT1", "yT_r"] + ["yT_m%d" % a for a in range(4)]
            for si, (r0, nr) in enumerate(subs):
                hb = si % 2
                hk = "hbuf%d" % hb
                ph.op("sp", lambda e: e.dma_start(out=hbuf[0:nr, hb, :], in_=hsrc(si, nr)), writes=[hk], dma=hk)
                for half in range(2):
                    bank = 3 + half
                    for c in range(8):
                        ph.op("pe", lambda e: e.matmul(PS[bank][0:nr, :], lhsT=yT[:, c, r0:r0 + nr], rhs=wout[:, c, half * 512:(half + 1) * 512],
                                                       start=(c == 0), stop=(c == 7)),
                              reads=ykeys + ["wout"], writes=["PS%d" % bank])
                    ph.op("dve", lambda e: e.tensor_tensor(out=hbuf[0:nr, hb, half * 512:(half + 1) * 512], in0=PS[bank][0:nr, :],
                                                           in1=hbuf[0:nr, hb, half * 512:(half + 1) * 512], op=ALU.add),
                          reads=["PS%d" % bank, hk], writes=[hk])
                if meta:
                    ph.op("sp", lambda e: e.dma_start(out=hmeta_d[:, :], in_=hbuf[0:nr, hb, :]), reads=[hk], dma="h_st%d" % hb)
                else:
                    ph.op("sp", lambda e: e.dma_start(out=out_d[ti * 512 + si * 128:ti * 512 + si * 128 + nr, :], in_=hbuf[0:nr, hb, :]),
                          reads=[hk], dma="h_st%d" % hb)

        for l in range(DEPTH):
            load_weights(l)
            if _DBG <= 0:
                continue
            tiles = list(range(-1, NT))
            for g0 in range(0, len(tiles), TILES_PER_BLOCK):
                ph = Phase(nc)
                for ti in tiles[g0:g0 + TILES_PER_BLOCK]:
                    tile_program(l, ti, ph)
                ph.emit()
    return nc


_CACHE = {}


def _prep_inputs(x, meta_tokens, ln_w, w_in, w_out, conv_w, conv_b, q_a_norm, w_uq, kv_a_norm, w_ukv,
                 q_norm, k_norm, ret_norm):
    DEPTH = w_in.shape[0]
    SEQ = x.shape[1]
    f = np.float32
    sp = np.zeros((128, DEPTH, NSP), f)
    for l in range(DEPTH):
        sp[:, l, SP_LNW:SP_LNW + 8] = np.asarray(ln_w[l], f).reshape(8, 128).T
        sp[:, l, SP_QAN:SP_QAN + 2] = np.asarray(q_a_norm[l], f).reshape(2, 128).T
        sp[:, l, SP_KVAN] = np.asarray(kv_a_norm[l], f)
        cw = np.asarray(conv_w[l], f)
        for a in range(2):
            sp[:, l, SP_CW + a * 3:SP_CW + a * 3 + 3] = cw[:, a * 128:(a + 1) * 128].T
            sp[:, l, SP_CB + a] = np.asarray(conv_b[l], f)[a * 128:(a + 1) * 128]
        sp[:, l, SP_WQ:SP_WQ + 96] = np.asarray(q_norm[l], f)[None, :]
        sp[:, l, SP_WK:SP_WK + 96] = np.asarray(k_norm[l], f)[None, :]
        sp[:, l, SP_WR:SP_WR + 64] = np.asarray(ret_norm[l], f)[None, :]
    common = {
        "meta": np.ascontiguousarray(np.asarray(meta_tokens, f)),
        "w_in": np.ascontiguousarray(np.asarray(w_in, f)[:, :, _PERM]),
        "w_out": np.ascontiguousarray(np.asarray(w_out, f)),
        "w_uq": np.ascontiguousarray(np.asarray(w_uq, f)),
        "w_ukv": np.ascontiguousarray(np.asarray(w_ukv, f)),
        "sp": sp,
        "rt": _rope_table(N_META + SEQ),
        "rtab": _ret_tables(),
        "ident": np.eye(128, dtype=f),
    }
    return common


def kernel(x, meta_tokens, ln_w, w_in, w_out, conv_w, conv_b, q_a_norm, w_uq, kv_a_norm, w_ukv,
           q_norm, k_norm, ret_norm):
    x = np.asarray(x, np.float32)
    B, SEQ, _ = x.shape
    DEPTH = np.asarray(w_in).shape[0]
    key = (SEQ, DEPTH)
    if key not in _CACHE:
        _CACHE[key] = build_program(SEQ, DEPTH)
    nc = _CACHE[key]
    common = _prep_inputs(x, meta_tokens, ln_w, w_in, w_out, conv_w, conv_b, q_a_norm, w_uq, kv_a_norm, w_ukv,
                          q_norm, k_norm, ret_norm)
    in_maps = []
    for b in range(B):
        m = dict(common)
        m["x"] = np.ascontiguousarray(x[b])
        in_maps.append(m)
    res = run_bass_kernel_spmd(nc, in_maps, core_ids=list(range(B)))
    return np.stack([np.asarray(r["out"], np.float32) for r in res.results], axis=0)
```

```python
import contextlib
import os
import types
import numpy as np
import concourse.bass as bass
import concourse.mybir as mybir
from concourse.bass_utils import run_bass_kernel_spmd

F32 = mybir.dt.float32
BF16 = mybir.dt.bfloat16
AF = mybir.ActivationFunctionType
ALU = mybir.AluOpType
AX = mybir.AxisListType

D_MODEL = 1024
N_META = 16
EPS = 1e-6
D_IN = 2976
NF = 1536
NTM = 1440
NSP = 8 + 2 + 1 + 6 + 2 + 96 + 96 + 64
SP_LNW, SP_QAN, SP_KVAN, SP_CW, SP_CB, SP_WQ, SP_WK, SP_WR = 0, 8, 10, 11, 17, 19, 115, 211
KCH = 1024

ENGS = ("pe", "act", "dve", "pool", "sp")
_DBG = float(os.environ.get("K_DBG", "99"))
TILES_PER_BLOCK = int(os.environ.get("K_TPB", "4"))


class Inst:
    __slots__ = ("id", "eng", "fn", "deps", "dma", "semkey", "semval", "needed")

    def __init__(self, id, eng, fn, dma):
        self.id = id
        self.eng = eng
        self.fn = fn
        self.deps = []
        self.dma = dma
        self.semkey = None
        self.semval = 0
        self.needed = False


def _freeze(fn):
    if fn.__closure__ is None:
        return fn
    cells = []
    for c in fn.__closure__:
        try:
            cells.append(types.CellType(c.cell_contents))
        except ValueError:
            cells.append(c)
    return types.FunctionType(fn.__code__, fn.__globals__, fn.__name__, fn.__defaults__, tuple(cells))


class Phase:
    _uid = 0

    def __init__(self, nc, sync_same=True):
        self.nc = nc
        self.insts = []
        self.by_eng = {e: [] for e in ENGS}
        self.last_w = {}
        self.reads = {}
        self.sync_same = sync_same
        self.dma_keys = []
        self.dma_cum = {}

    def op(self, eng, fn, reads=(), writes=(), dma=None):
        ins = Inst(len(self.insts), eng, _freeze(fn), dma)
        deps = set()
        for k in reads:
            w = self.last_w.get(k)
            if w is not None:
                deps.add(w)
        for k in writes:
            w = self.last_w.get(k)
            if w is not None:
                deps.add(w)
            for r in self.reads.get(k, ()):
                deps.add(r)
        deps.discard(ins.id)
        ins.deps = [(d, self.dma_cum.get(self.insts[d].dma) if self.insts[d].dma is not None else None)
                    for d in sorted(deps)]
        for k in reads:
            self.reads.setdefault(k, []).append(ins.id)
        for k in writes:
            self.last_w[k] = ins.id
            self.reads[k] = []
        if dma is not None:
            if dma not in self.dma_keys:
                self.dma_keys.append(dma)
            self.dma_cum[dma] = self.dma_cum.get(dma, 0) + 16
        self.insts.append(ins)
        self.by_eng[eng].append(ins)
        return ins

    def emit(self):
        nc = self.nc
        insts = self.insts
        for ins in insts:
            for d, _v in ins.deps:
                di = insts[d]
                if di.dma is not None or di.eng != ins.eng or (self.sync_same and ins.eng != "pe"):
                    di.needed = True
        for ins in insts:
            if ins.dma is not None:
                ins.needed = True
        cnt = {}
        for e in ENGS:
            c = 0
            for ins in self.by_eng[e]:
                if ins.dma is not None:
                    k = ("dma", ins.dma)
                    cnt[k] = cnt.get(k, 0) + 16
                    ins.semkey = k
                    ins.semval = cnt[k]
                elif ins.needed:
                    c += 1
                    ins.semkey = ("eng", e)
                    ins.semval = c
        semkeys = [("eng", e) for e in ENGS] + [("dma", k) for k in self.dma_keys]
        dma_final = {("dma", k): cnt.get(("dma", k), 0) for k in self.dma_keys}
        with contextlib.ExitStack() as st:
            st.enter_context(nc.cleanup_on_exit())
            sems = {}
            for k in semkeys:
                Phase._uid += 1
                sems[k] = nc.alloc_semaphore("s%d_%s_%s" % (Phase._uid, k[0], str(k[1])))
            block = st.enter_context(nc.Block())

            def body_for(e):
                def body(engobj):
                    waited = {}
                    for ins in self.by_eng[e]:
                        for d, dv in ins.deps:
                            di = insts[d]
                            if di.semkey is None:
                                continue
                            if di.dma is None and di.eng == e and (e == "pe" or not self.sync_same):
                                continue
                            val = dv if dv is not None else di.semval
                            if waited.get(di.semkey, 0) >= val:
                                continue
                            engobj.wait_ge(sems[di.semkey], val)
                            waited[di.semkey] = val
                        bi = ins.fn(engobj)
                        if ins.needed:
                            bi.then_inc(sems[ins.semkey], 16 if ins.dma is not None else 1)
                    mine = set(i.semkey for i in self.by_eng[e] if i.dma is not None)
                    for k in mine:
                        if waited.get(k, 0) < dma_final[k]:
                            engobj.wait_ge(sems[k], dma_final[k])
                return body

            block.tensor(body_for("pe"))
            block.scalar(body_for("act"))
            block.vector(body_for("dve"))
            block.gpsimd(body_for("pool"))
            block.sync(body_for("sp"))


def _rope_table(L):
    def tab(dim):
        inv = (1.0 / (np.float32(10000.0) ** (np.arange(0, dim, 2, dtype=np.float32) / np.float32(dim)))).astype(np.float32)
        ang = (np.arange(L, dtype=np.float32)[:, None] * inv[None, :]).astype(np.float32)
        return np.cos(ang).astype(np.float32), np.sin(ang).astype(np.float32)
    cm, sm = tab(32)
    cr, sr = tab(64)
    return np.ascontiguousarray(np.concatenate([cm, sm, cr, sr], axis=1).astype(np.float32))


def _ret_tables():
    H = 4
    log_g = np.log1p(-np.exp2(-5.0 - np.arange(H, dtype=np.float64)))
    idx = np.arange(64, dtype=np.float64)
    intra = np.exp(log_g[:, None, None] * np.abs(idx[:, None] - idx[None, :]))
    kvd = np.exp(log_g[:, None] * (63.0 - idx)[None, :])
    qd = np.exp(log_g[:, None] * (idx + 1.0)[None, :])
    cd = np.exp(log_g * 64.0)
    sc = 64.0 ** -0.5
    D2 = np.zeros((128, H, 128), np.float64)
    for a in range(2):
        D2[a * 64:(a + 1) * 64, :, a * 64:(a + 1) * 64] = np.transpose(intra, (1, 0, 2)) * sc
    QDA = np.zeros((128, H, 128), np.float64)
    QDB = np.zeros((128, H, 128), np.float64)
    QDA[:, :, 0:64] = qd[None, :, :]
    QDB[:, :, 64:128] = qd[None, :, :]
    G = np.zeros((128, 2, 64), np.float64)
    for r in range(128):
        for p in range(2):
            G[r, p, :] = cd[2 * p + r // 64]
    tab = np.zeros((128, RT_N), np.float32)
    tab[:, RT_D2:RT_D2 + 512] = D2.reshape(128, 512)
    tab[:, RT_QDA:RT_QDA + 512] = QDA.reshape(128, 512)
    tab[:, RT_QDB:RT_QDB + 512] = QDB.reshape(128, 512)
    tab[:, RT_G:RT_G + 128] = G.reshape(128, 128)
    tab[0:64, RT_KVDA:RT_KVDA + 4] = kvd.T * sc
    tab[64:128, RT_KVDB:RT_KVDB + 4] = kvd.T * sc
    tab[0:16, RT_KVDM:RT_KVDM + 4] = (kvd.T * sc)[48:64]
    return tab


RT_D2, RT_QDA, RT_QDB, RT_G, RT_KVDA, RT_KVDB, RT_KVDM, RT_N = 0, 512, 1024, 1536, 1664, 1668, 1672, 1676

_OFF = np.cumsum([0, 256, 256, 256, 256, 256, 128, 32, 512, 256, 256, 256, 256])
_PERM = np.concatenate([np.arange(_OFF[0], _OFF[4]), np.arange(_OFF[7], _OFF[8]),
                        np.arange(_OFF[4], _OFF[7]), np.arange(_OFF[8], _OFF[12])])


def build_program(SEQ, DEPTH):
    assert SEQ % 512 == 0
    NT = SEQ // 512
    NB = SEQ // 128
    L = N_META + SEQ
    nc = bass.Bass("TRN2", target_bir_lowering=False)

    def din(name, shape, dt=F32):
        return nc.dram_tensor(name, list(shape), dt, kind="ExternalInput").ap()

    x_d = din("x", [SEQ, D_MODEL])
    meta_d = din("meta", [N_META, D_MODEL])
    win_d = din("w_in", [DEPTH, D_MODEL, D_IN])
    wout_d = din("w_out", [DEPTH, D_MODEL, D_MODEL])
    wuq_d = din("w_uq", [DEPTH, 256, 768])
    wukv_d = din("w_ukv", [DEPTH, 128, 1024])
    sp_d = din("sp", [128, DEPTH, NSP])
    rt_d = din("rt", [L, 96])
    rtab_d = din("rtab", [128, RT_N])
    ident_d = din("ident", [128, 128])
    out_d = nc.dram_tensor("out", [SEQ, D_MODEL], F32, kind="ExternalOutput").ap()
    hmeta_d = nc.dram_tensor("hmeta", [N_META, D_MODEL], F32).ap()
    kc_d = nc.dram_tensor("kc", [8, 96, SEQ], BF16).ap()
    vc_d = nc.dram_tensor("vc", [8, 128, NB, 128], BF16).ap()

    with contextlib.ExitStack() as st:
        def sb(name, shape, dt=F32):
            return st.enter_context(nc.sbuf_tensor("sb_" + name, list(shape), dt))

        def ps(name, shape, dt=F32):
            return st.enter_context(nc.psum_tensor(name, list(shape), dt))

        ident = sb("ident", [128, 128], BF16)
        spt = sb("spt", [128, DEPTH, NSP])
        rtab = sb("rtab", [128, RT_N])
        win = sb("win", [128, 8, D_IN], BF16)
        wout = sb("wout", [128, 8, D_MODEL], BF16)
        wuq = sb("wuq", [128, 2, 768], BF16)
        wukv = sb("wukv", [128, 1024], BF16)
        wq_s = sb("wq_s", [128, 96])
        state = sb("state", [128, 2, 64])
        vconv = sb("vconv", [128, 2, 514])
        kT_meta = sb("kT_meta", [96, 8, 16], BF16)
        v_meta = sb("v_meta", [16, 8, 128], BF16)
        v_cur = sb("v_cur", [128, 4, 8, 128], BF16)
        hbuf = sb("hbuf", [128, 2, D_MODEL])
        u_bf = sb("u_bf", [128, 2, D_MODEL], BF16)
        uT = sb("uT", [128, 8, 512], BF16)
        rt_t = sb("rt_t", [128, 4, 96])
        stats = sb("stats", [128, 4, 8])
        st2 = sb("st2", [128, 24])
        junk = sb("junk", [128, 256], BF16)
        yT = sb("yT", [128, 8, 512], BF16)
        gate = sb("gate", [128, 512])
        cacc = sb("cacc", [128, 512])
        proj_t = sb("proj_t", [128, NTM])
        cn_bf = sb("cn_bf", [128, 384], BF16)
        cT = sb("cT", [128, 3, 128], BF16)
        qw = sb("qw", [128, 8, 96])
        kw = sb("kw", [128, 8, 96])
        sqw = sb("sqw", [128, 768])
        sqk = sb("sqk", [128, 768])
        rtb = sb("rtb", [128, 1024])
        rw = sb("rw", [128, 2, 4, 8, 16])
        qk_bf = sb("qk_bf", [128, 2, 8, 96], BF16)
        QT_cur = sb("QT_cur", [96, 8, 512], BF16)
        kT_cur = sb("kT_cur", [96, 8, 512], BF16)
        rqk = sb("rqk", [128, 8, 64])
        rqk_bf = sb("rqk_bf", [128, 8, 64], BF16)
        kdec = sb("kdec", [128, 2, 4, 64], BF16)
        qm = sb("qm", [128, 4, 128], BF16)
        qdm = sb("qdm", [128, 2, 4, 128], BF16)
        rv_bf = sb("rv_bf", [128, 4, 64], BF16)
        rT = sb("rT", [128, 4, 128], BF16)
        SD = sb("SD", [128, 4, 128], BF16)
        st_bf = sb("st_bf", [128, 2, 2, 64], BF16)
        st_tmp = sb("st_tmp", [128, 2, 64])
        o_sb = sb("o_sb", [128, 4, 64])
        o_sq = sb("o_sq", [128, 4, 64])
        rg = sb("rg", [128, 256])
        yr_bf = sb("yr_bf", [128, 256], BF16)
        NKB = 3
        kbuf = [sb("kbuf%d" % i, [96, KCH], BF16) for i in range(NKB)]
        vbuf = [sb("vbuf%d" % i, [128, KCH // 128, 128], BF16) for i in range(NKB)]
        NPB = 3
        pbuf = [sb("pbuf%d" % i, [128, 512], BF16) for i in range(NPB)]
        recip = sb("recip", [128, 512])
        on_sb = sb("on_sb", [128, 512])
        PS = [ps("ps%d" % i, [128, 512]) for i in range(7)]
        PT = ps("pt", [128, 1024], BF16)
        wstage = hbuf[:].rearrange("p a d -> p (a d)")

        ph = Phase(nc)
        ph.op("pool", lambda e: e.dma_start(out=ident[:], in_=ident_d[:, :]), writes=["ident"], dma="ident")
        ph.op("sp", lambda e: e.dma_start(out=spt[:], in_=sp_d[:, :, :]), writes=["spt"], dma="spt")
        ph.op("sp", lambda e: e.dma_start(out=rtab[:], in_=rtab_d[:, :]), writes=["rtab"], dma="rtab")
        ph.op("pool", lambda e: e.memset(v_cur[:].rearrange("p s h c -> p (s h c)"), 1.0), writes=["v_cur"])
        ph.op("pool", lambda e: e.memset(v_meta[:].rearrange("p h c -> p (h c)"), 1.0), writes=["v_meta"])
        ph.op("pool", lambda e: e.memset(qm[:].rearrange("p h c -> p (h c)"), 0.0), writes=["qm"])
        ph.emit()

        def load_weights(l):
            ph = Phase(nc)
            wv = win_d[l].rearrange("(c p) n -> p c n", p=128)
            HW = D_IN // 2
            for c in range(8):
                for hf in range(2):
                    ph.op("sp", lambda e: e.dma_start(out=wstage[:, 0:HW], in_=wv[:, c, hf * HW:(hf + 1) * HW]), writes=["wstage"], dma="wstage")
                    if hf == 0:
                        ph.op("act", lambda e: e.activation(out=win[:, c, 0:HW], in_=wstage[:, 0:HW], func=AF.Copy,
                                                            scale=spt[:, l, SP_LNW + c:SP_LNW + c + 1]),
                              reads=["wstage", "spt"], writes=["win"])
                    else:
                        ph.op("dve", lambda e: e.tensor_scalar(out=win[:, c, HW:D_IN], in0=wstage[:, 0:HW],
                                                               scalar1=spt[:, l, SP_LNW + c:SP_LNW + c + 1], scalar2=None, op0=ALU.mult),
                              reads=["wstage", "spt"], writes=["win"])
            wo = wout_d[l].rearrange("(c p) n -> p c n", p=128)
            ph.op("pool", lambda e: e.dma_start(out=wout[:], in_=wo[:, :, :]), writes=["wout"], dma="wout")
            wq = wuq_d[l].rearrange("(c p) n -> p c n", p=128)
            for c in range(2):
                ph.op("sp", lambda e: e.dma_start(out=wstage[:, 0:768], in_=wq[:, c, :]), writes=["wstage"], dma="wstage")
                ph.op("dve", lambda e: e.tensor_scalar(out=wuq[:, c, :], in0=wstage[:, 0:768],
                                                       scalar1=spt[:, l, SP_QAN + c:SP_QAN + c + 1], scalar2=None, op0=ALU.mult),
                      reads=["wstage", "spt"], writes=["wuq"])
            ph.op("sp", lambda e: e.dma_start(out=wstage[:, 0:1024], in_=wukv_d[l]), writes=["wstage"], dma="wstage")
            ph.op("dve", lambda e: e.tensor_scalar(out=wukv[:], in0=wstage[:, 0:1024],
                                                   scalar1=spt[:, l, SP_KVAN:SP_KVAN + 1], scalar2=None, op0=ALU.mult),
                  reads=["wstage", "spt"], writes=["wukv"])
            ph.op("dve", lambda e: e.tensor_scalar(out=wq_s[:], in0=spt[:, l, SP_WQ:SP_WQ + 96], scalar1=float(96.0 ** -0.5),
                                                   scalar2=None, op0=ALU.mult), reads=["spt"], writes=["wq_s"])
            ph.op("pool", lambda e: e.memset(state[:].rearrange("p a b -> p (a b)"), 0.0), writes=["state"])
            ph.op("pool", lambda e: e.memset(vconv[:].rearrange("p a b -> p (a b)"), 0.0), writes=["vconv"])
            ph.emit()

        def rstd_ops(ph, src_ap, dst_ap, n, keys_r, keys_w):
            ph.op("act", lambda e: e.activation(out=dst_ap, in_=src_ap, func=AF.Ln, scale=1.0 / n, bias=EPS),
                  reads=keys_r, writes=keys_w)
            ph.op("act", lambda e: e.activation(out=dst_ap, in_=dst_ap, func=AF.Exp, scale=-0.5),
                  reads=keys_w, writes=keys_w)

        def silu_ops(ph, z_ap, z_keys, g_ap, g_key):
            ph.op("act", lambda e: e.activation(out=g_ap, in_=z_ap, func=AF.Exp, scale=-1.0), reads=z_keys, writes=[g_key])
            ph.op("act", lambda e: e.activation(out=g_ap, in_=g_ap, func=AF.Ln, bias=1.0), reads=[g_key], writes=[g_key])
            ph.op("act", lambda e: e.activation(out=g_ap, in_=g_ap, func=AF.Exp, scale=-1.0), reads=[g_key], writes=[g_key])
            ph.op("dve", lambda e: e.tensor_tensor(out=g_ap, in0=z_ap, in1=g_ap, op=ALU.mult), reads=list(z_keys) + [g_key], writes=[g_key])

        def tile_program(l, ti, ph):
            meta = ti < 0
            T = N_META if meta else 512
            subs = [(0, N_META)] if meta else [(s * 128, 128) for s in range(4)]
            pos0 = 0 if meta else N_META + ti * 512
            first_layer = (l == 0)

            def hsrc(si, nr):
                if meta:
                    return (meta_d if first_layer else hmeta_d)[:, :]
                src = x_d if first_layer else out_d
                return src[ti * 512 + si * 128:ti * 512 + si * 128 + nr, :]

            if meta:
                ph.op("sp", lambda e: e.dma_start(out=rt_t[0:T, 0, :], in_=rt_d[0:T, :]), writes=["rt_t"], dma="rt_t")
            else:
                rv = rt_d[pos0:pos0 + 512, :].rearrange("(s p) d -> p s d", p=128)
                ph.op("sp", lambda e: e.dma_start(out=rt_t[:], in_=rv), writes=["rt_t"], dma="rt_t")
            for si, (r0, nr) in enumerate(subs):
                hb = si % 2
                hk = "hbuf%d" % hb
                ph.op("sp", lambda e: e.dma_start(out=hbuf[0:nr, hb, :], in_=hsrc(si, nr)), writes=[hk], dma=hk)
                uk = "u_bf%d" % hb
                ph.op("act", lambda e: e.activation(out=u_bf[0:nr, hb, :], in_=hbuf[0:nr, hb, :], func=AF.Square, accum_out=stats[0:nr, si, 0:1]),
                      reads=[hk], writes=[uk, "stats"])
                rstd_ops(ph, stats[0:nr, si, 0:1], stats[0:nr, si, 1:2], float(D_MODEL), ["stats"], ["stats"])
                ph.op("act", lambda e: e.activation(out=u_bf[0:nr, hb, :], in_=hbuf[0:nr, hb, :], func=AF.Copy, scale=stats[0:nr, si, 1:2]),
                      reads=[hk, "stats"], writes=[uk])
                for c in range(8):
                    ph.op("pe", lambda e: e.transpose(out=PT[:, c * 128:c * 128 + nr], in_=u_bf[0:nr, hb, c * 128:(c + 1) * 128], identity=ident[0:nr, 0:nr]),
                          reads=[uk, "ident"], writes=["PT"])
                ph.op("dve", lambda e: e.tensor_copy(out=uT[:, :, r0:r0 + nr], in_=PT[:, :].rearrange("p (c t) -> p c t", c=8)[:, :, 0:nr]),
                      reads=["PT"], writes=["uT"])

            if _DBG <= 1:
                return

            def fmajor(ft, bank):
                for c in range(8):
                    ph.op("pe", lambda e: e.matmul(PS[bank][:, 0:T], lhsT=win[:, c, ft * 128:(ft + 1) * 128], rhs=uT[:, c, 0:T],
                                                   start=(c == 0), stop=(c == 7)),
                          reads=["win", "uT"], writes=["PS%d" % bank])

            def conv_gen(a):
                cwb = SP_CW + a * 3
                fmajor(0 + a, 1)
                yield
                ph.op("act", lambda e: e.activation(out=cacc[:, 0:T], in_=PS[1][:, 0:T], func=AF.Copy), reads=["PS1"], writes=["cacc"])
                yield
                fmajor(4 + a, 1)
                yield
                ph.op("dve", lambda e: e.tensor_tensor(out=vconv[:, a, 2:2 + T], in0=PS[1][:, 0:T], in1=cacc[:, 0:T], op=ALU.mult),
                      reads=["PS1", "cacc"], writes=["vconv"])
                yield
                fmajor(2 + a, 1)
                ph.op("dve", lambda e: e.tensor_scalar(out=cacc[:, 0:T], in0=vconv[:, a, 0:T], scalar1=spt[:, l, cwb:cwb + 1],
                                                       scalar2=spt[:, l, SP_CB + a:SP_CB + a + 1], op0=ALU.mult, op1=ALU.add),
                      reads=["vconv", "spt"], writes=["cacc"])
                yield
                for j in (1, 2):
                    ph.op("dve", lambda e: e.scalar_tensor_tensor(out=cacc[:, 0:T], in0=vconv[:, a, j:j + T], scalar=spt[:, l, cwb + j:cwb + j + 1],
                                                                  in1=cacc[:, 0:T], op0=ALU.mult, op1=ALU.add),
                          reads=["vconv", "spt", "cacc"], writes=["cacc"])
                    yield
                ph.op("pool", lambda e: e.tensor_copy(out=vconv[:, a, 0:2], in_=vconv[:, a, T:T + 2]), reads=["vconv"], writes=["vconv"])
                ph.op("dve", lambda e: e.tensor_tensor(out=cacc[:, 0:T], in0=PS[1][:, 0:T], in1=cacc[:, 0:T], op=ALU.mult),
                      reads=["PS1", "cacc"], writes=["cacc"])
                yield
                fmajor(6 + a, 1)
                yield
                silu_ops(ph, PS[1][:, 0:T], ["PS1"], gate[:, 0:T], "gate")
                yield
                ph.op("pool", lambda e: e.tensor_tensor(out=yT[:, a, 0:T], in0=cacc[:, 0:T], in1=gate[:, 0:T], op=ALU.mult),
                      reads=["cacc", "gate"], writes=["yT%d" % a])
                yield

            if _DBG <= 2:
                return
            def proj_gen(sj):
                rj, nj = subs[sj]
                groups = [(0, 512, 2), (512, 512, 3), (1024, NTM - 1024, 4)]
                for (c0, cn, bank) in groups:
                    for c in range(8):
                        ph.op("pe", lambda e: e.matmul(PS[bank][0:nj, 0:cn], lhsT=uT[:, c, rj:rj + nj], rhs=win[:, c, NF + c0:NF + c0 + cn],
                                                       start=(c == 0), stop=(c == 7)),
                              reads=["win", "uT"], writes=["PS%d" % bank])
                        if c % 4 == 3:
                            yield
                ph.op("act", lambda e: e.activation(out=proj_t[0:nj, 0:512], in_=PS[2][0:nj, 0:512], func=AF.Copy), reads=["PS2"], writes=["proj_a"])
                yield
                ph.op("dve", lambda e: e.tensor_copy(out=proj_t[0:nj, 512:1024], in_=PS[3][0:nj, 0:512]), reads=["PS3"], writes=["proj_b"])
                yield
                ph.op("act", lambda e: e.activation(out=proj_t[0:nj, 1024:NTM], in_=PS[4][0:nj, 0:NTM - 1024], func=AF.Copy), reads=["PS4"], writes=["proj_c"])
                yield

            for si, (r0, nr) in enumerate(subs):
                if si == 0:
                    for _ in proj_gen(0):
                        pass
                if _DBG <= 2.1:
                    continue
                ph.op("act", lambda e: e.activation(out=junk[0:nr, 0:256], in_=proj_t[0:nr, 0:256], func=AF.Square, accum_out=stats[0:nr, si, 2:3]),
                      reads=["proj_a"], writes=["junk", "stats"])
                ph.op("act", lambda e: e.activation(out=junk[0:nr, 0:128], in_=proj_t[0:nr, 256:384], func=AF.Square, accum_out=stats[0:nr, si, 3:4]),
                      reads=["proj_a"], writes=["junk", "stats"])
                rstd_ops(ph, stats[0:nr, si, 2:3], stats[0:nr, si, 2:3], 256.0, ["stats"], ["stats"])
                rstd_ops(ph, stats[0:nr, si, 3:4], stats[0:nr, si, 3:4], 128.0, ["stats"], ["stats"])
                ph.op("dve", lambda e: e.tensor_scalar(out=cn_bf[0:nr, 0:256], in0=proj_t[0:nr, 0:256], scalar1=stats[0:nr, si, 2:3], scalar2=None, op0=ALU.mult),
                      reads=["proj_a", "stats"], writes=["cn_bf"])
                ph.op("dve", lambda e: e.tensor_scalar(out=cn_bf[0:nr, 256:384], in0=proj_t[0:nr, 256:384], scalar1=stats[0:nr, si, 3:4], scalar2=None, op0=ALU.mult),
                      reads=["proj_a", "stats"], writes=["cn_bf"])
                for j in range(3):
                    ph.op("pe", lambda e: e.transpose(out=PT[:, j * 128:j * 128 + nr], in_=cn_bf[0:nr, j * 128:(j + 1) * 128], identity=ident[0:nr, 0:nr]),
                          reads=["cn_bf", "ident"], writes=["PT"])
                ph.op("dve", lambda e: e.tensor_copy(out=cT[:, :, 0:nr], in_=PT[:, 0:384].rearrange("p (j t) -> p j t", j=3)[:, :, 0:nr]),
                      reads=["PT"], writes=["cT"])
                for half in range(2):
                    for c in range(2):
                        ph.op("pe", lambda e: e.matmul(PS[half][0:nr, 0:384], lhsT=cT[:, c, 0:nr], rhs=wuq[:, c, half * 384:(half + 1) * 384],
                                                       start=(c == 0), stop=(c == 1)),
                              reads=["cT", "wuq"], writes=["PS%d" % half])
                for half in range(2):
                    ph.op("pe", lambda e: e.matmul(PS[2 + half][0:nr, 0:512], lhsT=cT[:, 2, 0:nr], rhs=wukv[:, half * 512:(half + 1) * 512],
                                                   start=True, stop=True),
                          reads=["cT", "wukv"], writes=["PS%d" % (2 + half)])
                if _DBG <= 2.2:
                    continue
                ph.op("act", lambda e: e.activation(out=qw[0:nr, 0:4, :], in_=PS[0][0:nr, 0:384].rearrange("p (h d) -> p h d", h=4), func=AF.Copy),
                      reads=["PS0"], writes=["qw"])
                ph.op("dve", lambda e: e.tensor_copy(out=qw[0:nr, 4:8, :], in_=PS[1][0:nr, 0:384].rearrange("p (h d) -> p h d", h=4)),
                      reads=["PS1"], writes=["qw"])
                if _DBG <= 2.21:
                    continue
                for half in range(2):
                    kvv = PS[2 + half][0:nr, 0:512].rearrange("p (h c) -> p h c", h=4)
                    if half == 0:
                        ph.op("dve", lambda e: e.tensor_copy(out=kw[0:nr, 0:4, 0:64], in_=kvv[:, :, 0:64]), reads=["PS2"], writes=["kw"])
                    else:
                        ph.op("dve", lambda e: e.tensor_copy(out=kw[0:nr, 4:8, 0:64], in_=kvv[:, :, 0:64]), reads=["PS3"], writes=["kw"])
                    if _DBG <= 2.22:
                        continue
                    vdst = (v_meta[0:nr, :, :] if meta else v_cur[0:nr, si, :, :])
                    for hh in range(4):
                        hd = half * 4 + hh
                        cdst = 0 if hd % 2 == 0 else 64
                        ph.op("dve" if hh % 2 == 0 else "act",
                              (lambda e: e.tensor_copy(out=vdst[:, hd, cdst:cdst + 64], in_=kvv[:, hh, 64:128])) if hh % 2 == 0 else
                              (lambda e: e.activation(out=vdst[:, hd, cdst:cdst + 64], in_=kvv[:, hh, 64:128], func=AF.Copy)),
                              reads=["PS%d" % (2 + half)], writes=["vnew"])
                ph.op("dve", lambda e: e.tensor_copy(out=kw[0:nr, :, 64:96], in_=proj_t[0:nr, 384:416].unsqueeze(1).to_broadcast([nr, 8, 32])),
                      reads=["proj_a"], writes=["kw"])
                if _DBG <= 2.3:
                    continue
                def qk_chain(which):
                    wt = qw if which == 0 else kw
                    gain_ap = wq_s[:, :] if which == 0 else spt[:, l, SP_WK:SP_WK + 96]
                    wk = "qw" if which == 0 else "kw"
                    sk = "st2_%d" % which
                    sqk_ = "sqw%d" % which
                    sap = st2[0:nr, which * 8:which * 8 + 8]
                    sqb = sqw if which == 0 else sqk
                    sq3 = sqb[0:nr, 0:768].rearrange("p (h d) -> p h d", h=8)
                    ph.op("act", lambda e: e.activation(out=sqb[0:nr, 0:768], in_=wt[0:nr].rearrange("p h d -> p (h d)"), func=AF.Square), reads=[wk], writes=[sqk_])
                    yield
                    ph.op("dve", lambda e: e.tensor_reduce(out=sap, in_=sq3, axis=AX.X, op=ALU.add), reads=[sqk_], writes=[sk])
                    yield
                    rstd_ops(ph, sap, sap, 96.0, [sk], [sk])
                    yield
                    ph.op("dve", lambda e: e.tensor_tensor(out=wt[0:nr], in0=wt[0:nr], in1=sap.unsqueeze(2).to_broadcast([nr, 8, 96]), op=ALU.mult),
                          reads=[wk, sk], writes=[wk])
                    yield
                    ph.op("pool", lambda e: e.tensor_tensor(out=wt[0:nr], in0=wt[0:nr], in1=gain_ap[0:nr].unsqueeze(1).to_broadcast([nr, 8, 96]), op=ALU.mult),
                          reads=[wk, "wq_s", "spt"], writes=[wk])
                    yield
                    cosb = rt_t[0:nr, si, 0:16].unsqueeze(1).to_broadcast([nr, 8, 16])
                    sinb = rt_t[0:nr, si, 16:32].unsqueeze(1).to_broadcast([nr, 8, 16])
                    x1 = wt[0:nr, :, 64:80]
                    x2 = wt[0:nr, :, 80:96]
                    rk_ = "rw%d_" % which
                    ph.op("pool", lambda e: e.tensor_tensor(out=rw[0:nr, which, 0], in0=x1, in1=cosb, op=ALU.mult), reads=[wk, "rt_t"], writes=[rk_ + "0"])
                    ph.op("dve", lambda e: e.tensor_tensor(out=rw[0:nr, which, 2], in0=x1, in1=sinb, op=ALU.mult), reads=[wk, "rt_t"], writes=[rk_ + "2"])
                    yield
                    ph.op("pool", lambda e: e.tensor_tensor(out=rw[0:nr, which, 1], in0=x2, in1=sinb, op=ALU.mult), reads=[wk, "rt_t"], writes=[rk_ + "1"])
                    ph.op("dve", lambda e: e.tensor_tensor(out=rw[0:nr, which, 3], in0=x2, in1=cosb, op=ALU.mult), reads=[wk, "rt_t"], writes=[rk_ + "3"])
                    yield
                    ph.op("pool", lambda e: e.tensor_tensor(out=x1, in0=rw[0:nr, which, 0], in1=rw[0:nr, which, 1], op=ALU.subtract),
                          reads=[rk_ + "0", rk_ + "1", wk], writes=[wk])
                    ph.op("dve", lambda e: e.tensor_tensor(out=x2, in0=rw[0:nr, which, 2], in1=rw[0:nr, which, 3], op=ALU.add),
                          reads=[rk_ + "2", rk_ + "3", wk], writes=[wk])
                    yield
                    ph.op("act", lambda e: e.activation(out=qk_bf[0:nr, which].rearrange("p h d -> p (h d)"), in_=wt[0:nr].rearrange("p h d -> p (h d)"), func=AF.Copy),
                          reads=[wk], writes=["qk_bf%d" % which])
                    yield
                    if _DBG <= 2.4:
                        return
                    for h in range(8):
                        ph.op("pe", lambda e: e.transpose(out=PT[0:96, h * 128:h * 128 + nr], in_=qk_bf[0:nr, which, h, :], identity=ident[0:nr, 0:nr]),
                              reads=["qk_bf%d" % which, "ident"], writes=["PT"])
                    if meta and which == 1:
                        dstT, dkey = kT_meta[:, :, 0:nr], "kT_meta"
                    else:
                        dstT, dkey = (QT_cur if which == 0 else kT_cur)[:, :, r0:r0 + nr], ("QT_cur" if which == 0 else "kT_cur")
                    srcT = PT[0:96, :].rearrange("p (h t) -> p h t", h=8)[:, :, 0:nr]
                    ph.op("dve", lambda e: e.tensor_copy(out=dstT, in_=srcT), reads=["PT"], writes=[dkey])
                    yield

                qk = proj_t[0:nr, 416:928].rearrange("p (h d) -> p h d", h=8)
                cosr = rt_t[0:nr, si, 32:64].unsqueeze(1).to_broadcast([nr, 8, 32])
                sinr = rt_t[0:nr, si, 64:96].unsqueeze(1).to_broadcast([nr, 8, 32])
                rtmp = rtb[0:nr, :].rearrange("p (a h d) -> p a h d", a=4, h=8)
                pk = ["proj_a", "proj_b", "rt_t"]
                if _DBG > 3:
                    ph.op("pool", lambda e: e.tensor_tensor(out=rtmp[:, 0], in0=qk[:, :, 0:32], in1=cosr, op=ALU.mult), reads=pk, writes=["rt0"])
                    ph.op("dve", lambda e: e.tensor_tensor(out=rtmp[:, 2], in0=qk[:, :, 0:32], in1=sinr, op=ALU.mult), reads=pk, writes=["rt2"])
                    ph.op("pool", lambda e: e.tensor_tensor(out=rtmp[:, 1], in0=qk[:, :, 32:64], in1=sinr, op=ALU.mult), reads=pk, writes=["rt1"])
                    ph.op("dve", lambda e: e.tensor_tensor(out=rtmp[:, 3], in0=qk[:, :, 32:64], in1=cosr, op=ALU.mult), reads=pk, writes=["rt3"])
                    ph.op("act", lambda e: e.activation(out=rv_bf[0:nr].rearrange("p h d -> p (h d)"), in_=proj_t[0:nr, 928:1184], func=AF.Copy),
                          reads=["proj_b", "proj_c"], writes=["rv_bf"])
                    silu_ops(ph, proj_t[0:nr, 1184:1440], ["proj_c"], rg[0:nr, :], "rg")

                def ret_chain():
                    ph.op("pool", lambda e: e.tensor_tensor(out=rqk[0:nr, :, 0:32], in0=rtmp[:, 0], in1=rtmp[:, 1], op=ALU.subtract), reads=["rt0", "rt1"], writes=["rqk"])
                    ph.op("dve", lambda e: e.tensor_tensor(out=rqk[0:nr, :, 32:64], in0=rtmp[:, 2], in1=rtmp[:, 3], op=ALU.add), reads=["rt2", "rt3", "rqk"], writes=["rqk"])
                    yield
                    ph.op("act", lambda e: e.activation(out=rqk_bf[0:nr].rearrange("p h d -> p (h d)"), in_=rqk[0:nr].rearrange("p h d -> p (h d)"), func=AF.Copy),
                          reads=["rqk"], writes=["rqk_bf"])
                    yield
                    for ch in range(1 if meta else 2):
                        kcol = RT_KVDM if meta else (RT_KVDA if ch == 0 else RT_KVDB)
                        ph.op("pool", lambda e: e.tensor_tensor(out=kdec[0:nr, ch], in0=rqk[0:nr, 4:8, :],
                                                                in1=rtab[0:nr, kcol:kcol + 4].unsqueeze(2).to_broadcast([nr, 4, 64]), op=ALU.mult),
                              reads=["rqk", "rtab"], writes=["kdec%d" % ch])
                        yield
                    for j in range(4):
                        ph.op("pe", lambda e: e.transpose(out=PT[:, j * 128:j * 128 + nr], in_=rqk_bf[0:nr, 2 * j:2 * j + 2, :].rearrange("p a b -> p (a b)"),
                                                          identity=ident[0:nr, 0:nr]),
                              reads=["rqk_bf", "ident"], writes=["PT"])
                    ph.op("dve", lambda e: e.tensor_copy(out=rT[:, :, 0:nr], in_=PT[:, 0:512].rearrange("p (j t) -> p j t", j=4)[:, :, 0:nr]), reads=["PT"], writes=["rT"])
                    yield
                    ph.op("pool", lambda e: e.tensor_copy(out=qm[0:64, 0:4:2, 0:nr], in_=rT[0:64, 0:2, 0:nr]), reads=["rT"], writes=["qm"])
                    ph.op("pool", lambda e: e.tensor_copy(out=qm[64:128, 1:4:2, 0:nr], in_=rT[64:128, 0:2, 0:nr]), reads=["rT", "qm"], writes=["qm"])
                    yield
                    for h in range(4):
                        ph.op("pe", lambda e: e.matmul(PS[5][0:nr, h * 128:h * 128 + nr], lhsT=rT[:, 2 + h // 2, 0:nr], rhs=qm[:, h, 0:nr], start=True, stop=True),
                              reads=["rT", "qm"], writes=["PS5"])
                    ph.op("dve", lambda e: e.tensor_tensor(out=SD[0:nr, :, 0:nr], in0=PS[5][0:nr, :].rearrange("p (h c) -> p h c", h=4)[:, :, 0:nr],
                                                           in1=rtab[0:nr, RT_D2:RT_D2 + 512].rearrange("p (h c) -> p h c", h=4)[:, :, 0:nr], op=ALU.mult),
                          reads=["PS5", "rtab"], writes=["SD"])
                    yield
                    if not meta:
                        for ch in range(2):
                            qcol = RT_QDA if ch == 0 else RT_QDB
                            ph.op("pool", lambda e: e.tensor_tensor(out=qdm[:, ch], in0=qm[:], in1=rtab[:, qcol:qcol + 512].rearrange("p (h c) -> p h c", h=4), op=ALU.mult),
                                  reads=["qm", "rtab"], writes=["qdm%d" % ch])
                            yield
                        ph.op("pool", lambda e: e.tensor_copy(out=st_bf[:, 0], in_=state[:]), reads=["state"], writes=["st_bf0"])
                        yield
                    nch = 1 if meta else 2
                    for ch in range(nch):
                        for pr in range(2):
                            ph.op("pe", lambda e: e.matmul(PS[6][:, (ch * 2 + pr) * 128:(ch * 2 + pr + 1) * 128],
                                                           lhsT=kdec[0:nr, ch, 2 * pr:2 * pr + 2, :].rearrange("p a b -> p (a b)"),
                                                           rhs=rv_bf[0:nr, 2 * pr:2 * pr + 2, :].rearrange("p a b -> p (a b)"), start=True, stop=True),
                                  reads=["kdec%d" % ch, "rv_bf"], writes=["PS6"])
                        ph.op("pool", lambda e: e.tensor_tensor(out=st_tmp[:], in0=state[:], in1=rtab[:, RT_G:RT_G + 128].rearrange("p (a c) -> p a c", a=2), op=ALU.mult),
                              reads=["state", "rtab"], writes=["st_tmp"])
                        yield
                        kvv6 = PS[6][:, ch * 256:ch * 256 + 256].rearrange("p (a c) -> p a c", a=2)
                        ph.op("dve", lambda e: e.tensor_tensor(out=state[0:64], in0=kvv6[0:64, :, 0:64], in1=st_tmp[0:64], op=ALU.add),
                              reads=["PS6", "st_tmp"], writes=["state"])
                        ph.op("dve", lambda e: e.tensor_tensor(out=state[64:128], in0=kvv6[64:128, :, 64:128], in1=st_tmp[64:128], op=ALU.add),
                              reads=["PS6", "st_tmp", "state"], writes=["state"])
                        yield
                        if ch == 0 and not meta:
                            ph.op("pool", lambda e: e.tensor_copy(out=st_bf[:, 1], in_=state[:]), reads=["state"], writes=["st_bf1"])
                            yield
                    for h in range(4):
                        ph.op("pe", lambda e: e.matmul(PS[0][0:nr, h * 64:(h + 1) * 64], lhsT=SD[0:nr, h, 0:nr], rhs=rv_bf[0:nr, h, :], start=True, stop=meta),
                              reads=["SD", "rv_bf"], writes=["PS0"])
                        if not meta:
                            for ch in range(2):
                                ph.op("pe", lambda e: e.matmul(PS[0][:, h * 64:(h + 1) * 64], lhsT=qdm[:, ch, h, :], rhs=st_bf[:, ch, h // 2, :],
                                                               start=False, stop=(ch == 1)),
                                      reads=["qdm0", "qdm1", "st_bf0", "st_bf1"], writes=["PS0"])
                    rs = st2[0:nr, 16:20]
                    ph.op("act", lambda e: e.activation(out=o_sb[0:nr].rearrange("p h d -> p (h d)"), in_=PS[0][0:nr, 0:256], func=AF.Copy), reads=["PS0"], writes=["o_sb"])
                    yield
                    ph.op("act", lambda e: e.activation(out=o_sq[0:nr].rearrange("p h d -> p (h d)"), in_=o_sb[0:nr].rearrange("p h d -> p (h d)"), func=AF.Square),
                          reads=["o_sb"], writes=["o_sq"])
                    yield
                    ph.op("dve", lambda e: e.tensor_reduce(out=rs, in_=o_sq[0:nr], axis=AX.X, op=ALU.add), reads=["o_sq"], writes=["st2_r"])
                    yield
                    rstd_ops(ph, rs, rs, 64.0, ["st2_r"], ["st2_r"])
                    yield
                    ph.op("dve", lambda e: e.tensor_tensor(out=o_sb[0:nr], in0=o_sb[0:nr], in1=rs.unsqueeze(2).to_broadcast([nr, 4, 64]), op=ALU.mult),
                          reads=["o_sb", "st2_r"], writes=["o_sb"])
                    yield
                    ph.op("pool", lambda e: e.tensor_tensor(out=o_sb[0:nr], in0=o_sb[0:nr], in1=spt[0:nr, l, SP_WR:SP_WR + 64].unsqueeze(1).to_broadcast([nr, 4, 64]), op=ALU.mult),
                          reads=["o_sb", "spt"], writes=["o_sb"])
                    yield
                    ph.op("pool", lambda e: e.tensor_tensor(out=yr_bf[0:nr, :], in0=o_sb[0:nr].rearrange("p h d -> p (h d)"), in1=rg[0:nr, :], op=ALU.mult),
                          reads=["o_sb", "rg"], writes=["yr_bf"])
                    yield
                    for j in range(2):
                        ph.op("pe", lambda e: e.transpose(out=PT[:, j * 128:j * 128 + nr], in_=yr_bf[0:nr, j * 128:(j + 1) * 128], identity=ident[0:nr, 0:nr]),
                              reads=["yr_bf", "ident"], writes=["PT"])
                    ph.op("dve", lambda e: e.tensor_copy(out=yT[:, 6:8, r0:r0 + nr], in_=PT[:, 0:256].rearrange("p (j t) -> p j t", j=2)[:, :, 0:nr]), reads=["PT"], writes=["yT_r"])
                    yield

                gens = [qk_chain(0), qk_chain(1)] + ([ret_chain()] if _DBG > 3 else [])
                if si + 1 < len(subs):
                    gens.append(proj_gen(si + 1))
                if meta:
                    def conv_both():
                        yield from conv_gen(0)
                        yield from conv_gen(1)
                    gens.append(conv_both())
                elif si < 2:
                    gens.append(conv_gen(si))
                while gens:
                    for g in list(gens):
                        try:
                            next(g)
                        except StopIteration:
                            gens.remove(g)
            if _DBG <= 4:
                return
            if not meta:
                ph.op("sp", lambda e: e.dma_start(out=kc_d[:, :, ti * 512:(ti + 1) * 512].rearrange("h d t -> d h t"), in_=kT_cur[:]),
                      reads=["kT_cur"], writes=["kc_dram"], dma="kc_st")
                for h in range(8):
                    ph.op("sp", lambda e: e.dma_start(out=vc_d[h, :, ti * 4:(ti + 1) * 4, :], in_=v_cur[:, :, h, :]),
                          reads=["vnew"], writes=["vc_dram"], dma="vc_st")
            descs = []
            loads = []
            nload = 0
            for h in range(8):
                ob = 4 + (h % 3)
                hd = []
                if meta:
                    hd.append(dict(kT=kT_meta[:, h, 0:T], kk=["kT_meta"], v=v_meta[0:T, h, :], vk=["vnew"], nk=T, q0=0, diag=False))
                else:
                    hd.append(dict(kT=kT_meta[:, h, :], kk=["kT_meta"], v=v_meta[:, h, :], vk=["v_meta"], nk=N_META, q0=0, diag=False))
                    npast = ti * 512
                    for k0 in range(0, npast, KCH):
                        kn = min(KCH, npast - k0)
                        bi = nload % NKB
                        nload += 1

                        def ld(h=h, bi=bi, k0=k0, kn=kn):
                            ph.op("sp", lambda e: e.dma_start(out=kbuf[bi][:, 0:kn], in_=kc_d[h, :, k0:k0 + kn]),
                                  reads=["kc_dram"], writes=["kbuf%d" % bi], dma="kbuf%d" % bi)
                            ph.op("sp", lambda e: e.dma_start(out=vbuf[bi][:, 0:kn // 128, :], in_=vc_d[h, :, k0 // 128:(k0 + kn) // 128, :]),
                                  reads=["vc_dram"], writes=["vbuf%d" % bi], dma="vbuf%d" % bi)
                        loads.append((len(descs) + len(hd), ld))
                        for j in range(kn // 128):
                            hd.append(dict(kT=kbuf[bi][:, j * 128:(j + 1) * 128], kk=["kbuf%d" % bi], v=vbuf[bi][:, j, :], vk=["vbuf%d" % bi],
                                           nk=128, q0=0, diag=False))
                    for r in range(4):
                        hd.append(dict(kT=kT_cur[:, h, r * 128:(r + 1) * 128], kk=["kT_cur"], v=v_cur[:, r, h, :], vk=["vnew"], nk=128, q0=r * 128, diag=True))
                for i, d in enumerate(hd):
                    d.update(h=h, ob=ob, first=(i == 0), last=(i == len(hd) - 1))
                descs += hd

            SBK = (0, 1, 2)

            def emit_qk(i):
                d = descs[i]
                h, sbk, nq = d["h"], SBK[i % 3], T - d["q0"]
                kT_ap, nk, q0 = d["kT"], d["nk"], d["q0"]
                ph.op("pe", lambda e: e.matmul(PS[sbk][0:nk, 0:nq], lhsT=kT_ap, rhs=QT_cur[:, h, q0:T], start=True, stop=True),
                      reads=list(d["kk"]) + ["QT_cur"], writes=["PS%d" % sbk])

            def emit_pv(i):
                d = descs[i]
                h, ob, sbk, pb, nq = d["h"], d["ob"], SBK[i % 3], i % NPB, T - d["q0"]
                okey = "PS%d" % ob
                nk, q0, v_ap = d["nk"], d["q0"], d["v"]
                st_flag, last = d["first"], d["last"]
                ph.op("act", lambda e: e.activation(out=pbuf[pb][0:nk, 0:nq], in_=PS[sbk][0:nk, 0:nq], func=AF.Exp), reads=["PS%d" % sbk], writes=["pbuf%d" % pb])
                if d["diag"]:
                    ph.op("pool", lambda e: e.memset(pbuf[pb][64:128, 0:64], 0.0), reads=["pbuf%d" % pb], writes=["pbuf%d" % pb])
                ph.op("pe", lambda e: e.matmul(PS[ob][:, q0:T], lhsT=v_ap, rhs=pbuf[pb][0:nk, 0:nq], start=st_flag, stop=last),
                      reads=list(d["vk"]) + ["pbuf%d" % pb], writes=[okey])
                if last:
                    orow = slice(0, 64) if h % 2 == 0 else slice(64, 128)
                    srow = slice(64, 128) if h % 2 == 0 else slice(0, 64)
                    if ti < 0:
                        ph.op("act", lambda e: e.activation(out=recip[srow, 0:T], in_=PS[ob][srow, 0:T], func=AF.Ln), reads=[okey], writes=["recip%d" % (h % 2)])
                        ph.op("act", lambda e: e.activation(out=recip[srow, 0:T], in_=recip[srow, 0:T], func=AF.Exp, scale=-1.0),
                              reads=["recip%d" % (h % 2)], writes=["recip%d" % (h % 2)])
                    else:
                        ph.op("dve", lambda e: e.reciprocal(out=recip[srow, 0:T], in_=PS[ob][srow, 0:T]), reads=[okey], writes=["recip%d" % (h % 2)])
                    ph.op("dve", lambda e: e.tensor_tensor(out=on_sb[orow, 0:T], in0=PS[ob][orow, 0:T], in1=recip[srow, 0:T], op=ALU.mult),
                          reads=[okey, "recip%d" % (h % 2)], writes=["on_sb%d" % (h % 2)])
                    if h % 2 == 1:
                        ph.op("pool", lambda e: e.tensor_tensor(out=yT[:, 2 + h // 2, 0:T], in0=on_sb[:, 0:T], in1=gate[:, 0:T], op=ALU.mult),
                              reads=["on_sb0", "on_sb1", "gate"], writes=["yT_m%d" % (h // 2)])
                        if h < 7:
                            emit_gate(h // 2 + 1)

            def emit_gate(a):
                fmajor(8 + a, 3)
                silu_ops(ph, PS[3][:, 0:T], ["PS3"], gate[:, 0:T], "gate")

            emit_gate(0)
            LA, LL = 2, 12
            nd = len(descs)
            li = 0
            for i in range(nd + LA):
                while li < len(loads) and loads[li][0] <= i + LL:
                    loads[li][1]()
                    li += 1
                if i < nd:
                    emit_qk(i)
                if i - LA >= 0:
                    emit_pv(i - LA)
            if _DBG <= 5:
                return
            ykeys = ["yT0", "yT1", "yT_r"] + ["yT_m%d" % a for a in range(4)]
            for si, (r0, nr) in enumerate(subs):
                hb = si % 2
                hk = "hbuf%d" % hb
                ph.op("sp", lambda e: e.dma_start(out=hbuf[0:nr, hb, :], in_=hsrc(si, nr)), writes=[hk], dma=hk)
                for half in range(2):
                    bank = 3 + half
                    for c in range(8):
                        ph.op("pe", lambda e: e.matmul(PS[bank][0:nr, :], lhsT=yT[:, c, r0:r0 + nr], rhs=wout[:, c, half * 512:(half + 1) * 512],
                                                       start=(c == 0), stop=(c == 7)),
                              reads=ykeys + ["wout"], writes=["PS%d" % bank])
                    ph.op("dve", lambda e: e.tensor_tensor(out=hbuf[0:nr, hb, half * 512:(half + 1) * 512], in0=PS[bank][0:nr, :],
                                                           in1=hbuf[0:nr, hb, half * 512:(half + 1) * 512], op=ALU.add),
                          reads=["PS%d" % bank, hk], writes=[hk])
                if meta:
                    ph.op("sp", lambda e: e.dma_start(out=hmeta_d[:, :], in_=hbuf[0:nr, hb, :]), reads=[hk], dma="h_st%d" % hb)
                else:
                    ph.op("sp", lambda e: e.dma_start(out=out_d[ti * 512 + si * 128:ti * 512 + si * 128 + nr, :], in_=hbuf[0:nr, hb, :]),
                          reads=[hk], dma="h_st%d" % hb)

        for l in range(DEPTH):
            load_weights(l)
            if _DBG <= 0:
                continue
            tiles = list(range(-1, NT))
            for g0 in range(0, len(tiles), TILES_PER_BLOCK):
                ph = Phase(nc)
                for ti in tiles[g0:g0 + TILES_PER_BLOCK]:
                    tile_program(l, ti, ph)
                ph.emit()
    return nc


_CACHE = {}


def _prep_inputs(x, meta_tokens, ln_w, w_in, w_out, conv_w, conv_b, q_a_norm, w_uq, kv_a_norm, w_ukv,
                 q_norm, k_norm, ret_norm):
    DEPTH = w_in.shape[0]
    SEQ = x.shape[1]
    f = np.float32
    sp = np.zeros((128, DEPTH, NSP), f)
    for l in range(DEPTH):
        sp[:, l, SP_LNW:SP_LNW + 8] = np.asarray(ln_w[l], f).reshape(8, 128).T
        sp[:, l, SP_QAN:SP_QAN + 2] = np.asarray(q_a_norm[l], f).reshape(2, 128).T
        sp[:, l, SP_KVAN] = np.asarray(kv_a_norm[l], f)
        cw = np.asarray(conv_w[l], f)
        for a in range(2):
            sp[:, l, SP_CW + a * 3:SP_CW + a * 3 + 3] = cw[:, a * 128:(a + 1) * 128].T
            sp[:, l, SP_CB + a] = np.asarray(conv_b[l], f)[a * 128:(a + 1) * 128]
        sp[:, l, SP_WQ:SP_WQ + 96] = np.asarray(q_norm[l], f)[None, :]
        sp[:, l, SP_WK:SP_WK + 96] = np.asarray(k_norm[l], f)[None, :]
        sp[:, l, SP_WR:SP_WR + 64] = np.asarray(ret_norm[l], f)[None, :]
    common = {
        "meta": np.ascontiguousarray(np.asarray(meta_tokens, f)),
        "w_in": np.ascontiguousarray(np.asarray(w_in, f)[:, :, _PERM]),
        "w_out": np.ascontiguousarray(np.asarray(w_out, f)),
        "w_uq": np.ascontiguousarray(np.asarray(w_uq, f)),
        "w_ukv": np.ascontiguousarray(np.asarray(w_ukv, f)),
        "sp": sp,
        "rt": _rope_table(N_META + SEQ),
        "rtab": _ret_tables(),
        "ident": np.eye(128, dtype=f),
    }
    return common


def kernel(x, meta_tokens, ln_w, w_in, w_out, conv_w, conv_b, q_a_norm, w_uq, kv_a_norm, w_ukv,
           q_norm, k_norm, ret_norm):
    x = np.asarray(x, np.float32)
    B, SEQ, _ = x.shape
    DEPTH = np.asarray(w_in).shape[0]
    key = (SEQ, DEPTH)
    if key not in _CACHE:
        _CACHE[key] = build_program(SEQ, DEPTH)
    nc = _CACHE[key]
    common = _prep_inputs(x, meta_tokens, ln_w, w_in, w_out, conv_w, conv_b, q_a_norm, w_uq, kv_a_norm, w_ukv,
                          q_norm, k_norm, ret_norm)
    in_maps = []
    for b in range(B):
        m = dict(common)
        m["x"] = np.ascontiguousarray(x[b])
        in_maps.append(m)
    res = run_bass_kernel_spmd(nc, in_maps, core_ids=list(range(B)))
    return np.stack([np.asarray(r["out"], np.float32) for r in res.results], axis=0)
```

```python
import contextlib
import os
import types
import numpy as np
import concourse.bass as bass
import concourse.mybir as mybir
from concourse.bass_utils import run_bass_kernel_spmd

F32 = mybir.dt.float32
BF16 = mybir.dt.bfloat16
AF = mybir.ActivationFunctionType
ALU = mybir.AluOpType
AX = mybir.AxisListType

D_MODEL = 1024
N_META = 16
EPS = 1e-6
D_IN = 2976
NF = 1536
NTM = 1440
NSP = 8 + 2 + 1 + 6 + 2 + 96 + 96 + 64
SP_LNW, SP_QAN, SP_KVAN, SP_CW, SP_CB, SP_WQ, SP_WK, SP_WR = 0, 8, 10, 11, 17, 19, 115, 211
KCH = 1024

ENGS = ("pe", "act", "dve", "pool", "sp")
_DBG = float(os.environ.get("K_DBG", "99"))
TILES_PER_BLOCK = int(os.environ.get("K_TPB", "4"))


class Inst:
    __slots__ = ("id", "eng", "fn", "deps", "dma", "semkey", "semval", "needed")

    def __init__(self, id, eng, fn, dma):
        self.id = id
        self.eng = eng
        self.fn = fn
        self.deps = []
        self.dma = dma
        self.semkey = None
        self.semval = 0
        self.needed = False


def _freeze(fn):
    if fn.__closure__ is None:
        return fn
    cells = []
    for c in fn.__closure__:
        try:
            cells.append(types.CellType(c.cell_contents))
        except ValueError:
            cells.append(c)
    return types.FunctionType(fn.__code__, fn.__globals__, fn.__name__, fn.__defaults__, tuple(cells))


class Phase:
    _uid = 0

    def __init__(self, nc, sync_same=True):
        self.nc = nc
        self.insts = []
        self.by_eng = {e: [] for e in ENGS}
        self.last_w = {}
        self.reads = {}
        self.sync_same = sync_same
        self.dma_keys = []
        self.dma_cum = {}

    def op(self, eng, fn, reads=(), writes=(), dma=None):
        ins = Inst(len(self.insts), eng, _freeze(fn), dma)
        deps = set()
        for k in reads:
            w = self.last_w.get(k)
            if w is not None:
                deps.add(w)
        for k in writes:
            w = self.last_w.get(k)
            if w is not None:
                deps.add(w)
            for r in self.reads.get(k, ()):
                deps.add(r)
        deps.discard(ins.id)
        ins.deps = [(d, self.dma_cum.get(self.insts[d].dma) if self.insts[d].dma is not None else None)
                    for d in sorted(deps)]
        for k in reads:
            self.reads.setdefault(k, []).append(ins.id)
        for k in writes:
            self.last_w[k] = ins.id
            self.reads[k] = []
        if dma is not None:
            if dma not in self.dma_keys:
                self.dma_keys.append(dma)
            self.dma_cum[dma] = self.dma_cum.get(dma, 0) + 16
        self.insts.append(ins)
        self.by_eng[eng].append(ins)
        return ins

    def emit(self):
        nc = self.nc
        insts = self.insts
        for ins in insts:
            for d, _v in ins.deps:
                di = insts[d]
                if di.dma is not None or di.eng != ins.eng or (self.sync_same and ins.eng != "pe"):
                    di.needed = True
        for ins in insts:
            if ins.dma is not None:
                ins.needed = True
        cnt = {}
        for e in ENGS:
            c = 0
            for ins in self.by_eng[e]:
                if ins.dma is not None:
                    k = ("dma", ins.dma)
                    cnt[k] = cnt.get(k, 0) + 16
                    ins.semkey = k
                    ins.semval = cnt[k]
                elif ins.needed:
                    c += 1
                    ins.semkey = ("eng", e)
                    ins.semval = c
        semkeys = [("eng", e) for e in ENGS] + [("dma", k) for k in self.dma_keys]
        dma_final = {("dma", k): cnt.get(("dma", k), 0) for k in self.dma_keys}
        with contextlib.ExitStack() as st:
            st.enter_context(nc.cleanup_on_exit())
            sems = {}
            for k in semkeys:
                Phase._uid += 1
                sems[k] = nc.alloc_semaphore("s%d_%s_%s" % (Phase._uid, k[0], str(k[1])))
            block = st.enter_context(nc.Block())

            def body_for(e):
                def body(engobj):
                    waited = {}
                    for ins in self.by_eng[e]:
                        for d, dv in ins.deps:
                            di = insts[d]
                            if di.semkey is None:
                                continue
                            if di.dma is None and di.eng == e and (e == "pe" or not self.sync_same):
                                continue
                            val = dv if dv is not None else di.semval
                            if waited.get(di.semkey, 0) >= val:
                                continue
                            engobj.wait_ge(sems[di.semkey], val)
                            waited[di.semkey] = val
                        bi = ins.fn(engobj)
                        if ins.needed:
                            bi.then_inc(sems[ins.semkey], 16 if ins.dma is not None else 1)
                    mine = set(i.semkey for i in self.by_eng[e] if i.dma is not None)
                    for k in mine:
                        if waited.get(k, 0) < dma_final[k]:
                            engobj.wait_ge(sems[k], dma_final[k])
                return body

            block.tensor(body_for("pe"))
            block.scalar(body_for("act"))
            block.vector(body_for("dve"))
            block.gpsimd(body_for("pool"))
            block.sync(body_for("sp"))


def _rope_table(L):
    def tab(dim):
        inv = (1.0 / (np.float32(10000.0) ** (np.arange(0, dim, 2, dtype=np.float32) / np.float32(dim)))).astype(np.float32)
        ang = (np.arange(L, dtype=np.float32)[:, None] * inv[None, :]).astype(np.float32)
        return np.cos(ang).astype(np.float32), np.sin(ang).astype(np.float32)
    cm, sm = tab(32)
    cr, sr = tab(64)
    return np.ascontiguousarray(np.concatenate([cm, sm, cr, sr], axis=1).astype(np.float32))


def _ret_tables():
    H = 4
    log_g = np.log1p(-np.exp2(-5.0 - np.arange(H, dtype=np.float64)))
    idx = np.arange(64, dtype=np.float64)
    intra = np.exp(log_g[:, None, None] * np.abs(idx[:, None] - idx[None, :]))
    kvd = np.exp(log_g[:, None] * (63.0 - idx)[None, :])
    qd = np.exp(log_g[:, None] * (idx + 1.0)[None, :])
    cd = np.exp(log_g * 64.0)
    sc = 64.0 ** -0.5
    D2 = np.zeros((128, H, 128), np.float64)
    for a in range(2):
        D2[a * 64:(a + 1) * 64, :, a * 64:(a + 1) * 64] = np.transpose(intra, (1, 0, 2)) * sc
    QDA = np.zeros((128, H, 128), np.float64)
    QDB = np.zeros((128, H, 128), np.float64)
    QDA[:, :, 0:64] = qd[None, :, :]
    QDB[:, :, 64:128] = qd[None, :, :]
    G = np.zeros((128, 2, 64), np.float64)
    for r in range(128):
        for p in range(2):
            G[r, p, :] = cd[2 * p + r // 64]
    tab = np.zeros((128, RT_N), np.float32)
    tab[:, RT_D2:RT_D2 + 512] = D2.reshape(128, 512)
    tab[:, RT_QDA:RT_QDA + 512] = QDA.reshape(128, 512)
    tab[:, RT_QDB:RT_QDB + 512] = QDB.reshape(128, 512)
    tab[:, RT_G:RT_G + 128] = G.reshape(128, 128)
    tab[0:64, RT_KVDA:RT_KVDA + 4] = kvd.T * sc
    tab[64:128, RT_KVDB:RT_KVDB + 4] = kvd.T * sc
    tab[0:16, RT_KVDM:RT_KVDM + 4] = (kvd.T * sc)[48:64]
    return tab


RT_D2, RT_QDA, RT_QDB, RT_G, RT_KVDA, RT_KVDB, RT_KVDM, RT_N = 0, 512, 1024, 1536, 1664, 1668, 1672, 1676

_OFF = np.cumsum([0, 256, 256, 256, 256, 256, 128, 32, 512, 256, 256, 256, 256])
_PERM = np.concatenate([np.arange(_OFF[0], _OFF[4]), np.arange(_OFF[7], _OFF[8]),
                        np.arange(_OFF[4], _OFF[7]), np.arange(_OFF[8], _OFF[12])])


def build_program(SEQ, DEPTH):
    assert SEQ % 512 == 0
    NT = SEQ // 512
    NB = SEQ // 128
    L = N_META + SEQ
    nc = bass.Bass("TRN2", target_bir_lowering=False)

    def din(name, shape, dt=F32):
        return nc.dram_tensor(name, list(shape), dt, kind="ExternalInput").ap()

    x_d = din("x", [SEQ, D_MODEL])
    meta_d = din("meta", [N_META, D_MODEL])
    win_d = din("w_in", [DEPTH, D_MODEL, D_IN])
    wout_d = din("w_out", [DEPTH, D_MODEL, D_MODEL])
    wuq_d = din("w_uq", [DEPTH, 256, 768])
    wukv_d = din("w_ukv", [DEPTH, 128, 1024])
    sp_d = din("sp", [128, DEPTH, NSP])
    rt_d = din("rt", [L, 96])
    rtab_d = din("rtab", [128, RT_N])
    ident_d = din("ident", [128, 128])
    out_d = nc.dram_tensor("out", [SEQ, D_MODEL], F32, kind="ExternalOutput").ap()
    hmeta_d = nc.dram_tensor("hmeta", [N_META, D_MODEL], F32).ap()
    kc_d = nc.dram_tensor("kc", [8, 96, SEQ], BF16).ap()
    vc_d = nc.dram_tensor("vc", [8, 128, NB, 128], BF16).ap()

    with contextlib.ExitStack() as st:
        def sb(name, shape, dt=F32):
            return st.enter_context(nc.sbuf_tensor("sb_" + name, list(shape), dt))

        def ps(name, shape, dt=F32):
            return st.enter_context(nc.psum_tensor(name, list(shape), dt))

        ident = sb("ident", [128, 128], BF16)
        spt = sb("spt", [128, DEPTH, NSP])
        rtab = sb("rtab", [128, RT_N])
        win = sb("win", [128, 8, D_IN], BF16)
        wout = sb("wout", [128, 8, D_MODEL], BF16)
        wuq = sb("wuq", [128, 2, 768], BF16)
        wukv = sb("wukv", [128, 1024], BF16)
        wq_s = sb("wq_s", [128, 96])
        state = sb("state", [128, 2, 64])
        vconv = sb("vconv", [128, 2, 514])
        kT_meta = sb("kT_meta", [96, 8, 16], BF16)
        v_meta = sb("v_meta", [16, 8, 128], BF16)
        v_cur = sb("v_cur", [128, 4, 8, 128], BF16)
        hbuf = sb("hbuf", [128, 2, D_MODEL])
        u_bf = sb("u_bf", [128, 2, D_MODEL], BF16)
        uT = sb("uT", [128, 8, 512], BF16)
        rt_t = sb("rt_t", [128, 4, 96])
        stats = sb("stats", [128, 4, 8])
        st2 = sb("st2", [128, 24])
        junk = sb("junk", [128, 256], BF16)
        yT = sb("yT", [128, 8, 512], BF16)
        gate = sb("gate", [128, 512])
        cacc = sb("cacc", [128, 512])
        proj_t = sb("proj_t", [128, NTM])
        cn_bf = sb("cn_bf", [128, 384], BF16)
        cT = sb("cT", [128, 3, 128], BF16)
        qw = sb("qw", [128, 8, 96])
        kw = sb("kw", [128, 8, 96])
        sqw = sb("sqw", [128, 768])
        sqk = sb("sqk", [128, 768])
        rtb = sb("rtb", [128, 1024])
        rw = sb("rw", [128, 2, 4, 8, 16])
        qk_bf = sb("qk_bf", [128, 2, 8, 96], BF16)
        QT_cur = sb("QT_cur", [96, 8, 512], BF16)
        kT_cur = sb("kT_cur", [96, 8, 512], BF16)
        rqk = sb("rqk", [128, 8, 64])
        rqk_bf = sb("rqk_bf", [128, 8, 64], BF16)
        kdec = sb("kdec", [128, 2, 4, 64], BF16)
        qm = sb("qm", [128, 4, 128], BF16)
        qdm = sb("qdm", [128, 2, 4, 128], BF16)
        rv_bf = sb("rv_bf", [128, 4, 64], BF16)
        rT = sb("rT", [128, 4, 128], BF16)
        SD = sb("SD", [128, 4, 128], BF16)
        st_bf = sb("st_bf", [128, 2, 2, 64], BF16)
        st_tmp = sb("st_tmp", [128, 2, 64])
        o_sb = sb("o_sb", [128, 4, 64])
        o_sq = sb("o_sq", [128, 4, 64])
        rg = sb("rg", [128, 256])
        yr_bf = sb("yr_bf", [128, 256], BF16)
        NKB = 3
        kbuf = [sb("kbuf%d" % i, [96, KCH], BF16) for i in range(NKB)]
        vbuf = [sb("vbuf%d" % i, [128, KCH // 128, 128], BF16) for i in range(NKB)]
        NPB = 3
        pbuf = [sb("pbuf%d" % i, [128, 512], BF16) for i in range(NPB)]
        recip = sb("recip", [128, 512])
        on_sb = sb("on_sb", [128, 512])
        PS = [ps("ps%d" % i, [128, 512]) for i in range(7)]
        PT = ps("pt", [128, 1024], BF16)
        wstage = hbuf[:].rearrange("p a d -> p (a d)")

        ph = Phase(nc)
        ph.op("pool", lambda e: e.dma_start(out=ident[:], in_=ident_d[:, :]), writes=["ident"], dma="ident")
        ph.op("sp", lambda e: e.dma_start(out=spt[:], in_=sp_d[:, :, :]), writes=["spt"], dma="spt")
        ph.op("sp", lambda e: e.dma_start(out=rtab[:], in_=rtab_d[:, :]), writes=["rtab"], dma="rtab")
        ph.op("pool", lambda e: e.memset(v_cur[:].rearrange("p s h c -> p (s h c)"), 1.0), writes=["v_cur"])
        ph.op("pool", lambda e: e.memset(v_meta[:].rearrange("p h c -> p (h c)"), 1.0), writes=["v_meta"])
        ph.op("pool", lambda e: e.memset(qm[:].rearrange("p h c -> p (h c)"), 0.0), writes=["qm"])
        ph.emit()

        def load_weights(l):
            ph = Phase(nc)
            wv = win_d[l].rearrange("(c p) n -> p c n", p=128)
            HW = D_IN // 2
            for c in range(8):
                for hf in range(2):
                    ph.op("sp", lambda e: e.dma_start(out=wstage[:, 0:HW], in_=wv[:, c, hf * HW:(hf + 1) * HW]), writes=["wstage"], dma="wstage")
                    if hf == 0:
                        ph.op("act", lambda e: e.activation(out=win[:, c, 0:HW], in_=wstage[:, 0:HW], func=AF.Copy,
                                                            scale=spt[:, l, SP_LNW + c:SP_LNW + c + 1]),
                              reads=["wstage", "spt"], writes=["win"])
                    else:
                        ph.op("dve", lambda e: e.tensor_scalar(out=win[:, c, HW:D_IN], in0=wstage[:, 0:HW],
                                                               scalar1=spt[:, l, SP_LNW + c:SP_LNW + c + 1], scalar2=None, op0=ALU.mult),
                              reads=["wstage", "spt"], writes=["win"])
            wo = wout_d[l].rearrange("(c p) n -> p c n", p=128)
            ph.op("pool", lambda e: e.dma_start(out=wout[:], in_=wo[:, :, :]), writes=["wout"], dma="wout")
            wq = wuq_d[l].rearrange("(c p) n -> p c n", p=128)
            for c in range(2):
                ph.op("sp", lambda e: e.dma_start(out=wstage[:, 0:768], in_=wq[:, c, :]), writes=["wstage"], dma="wstage")
                ph.op("dve", lambda e: e.tensor_scalar(out=wuq[:, c, :], in0=wstage[:, 0:768],
                                                       scalar1=spt[:, l, SP_QAN + c:SP_QAN + c + 1], scalar2=None, op0=ALU.mult),
                      reads=["wstage", "spt"], writes=["wuq"])
            ph.op("sp", lambda e: e.dma_start(out=wstage[:, 0:1024], in_=wukv_d[l]), writes=["wstage"], dma="wstage")
            ph.op("dve", lambda e: e.tensor_scalar(out=wukv[:], in0=wstage[:, 0:1024],
                                                   scalar1=spt[:, l, SP_KVAN:SP_KVAN + 1], scalar2=None, op0=ALU.mult),
                  reads=["wstage", "spt"], writes=["wukv"])
            ph.op("dve", lambda e: e.tensor_scalar(out=wq_s[:], in0=spt[:, l, SP_WQ:SP_WQ + 96], scalar1=float(96.0 ** -0.5),
                                                   scalar2=None, op0=ALU.mult), reads=["spt"], writes=["wq_s"])
            ph.op("pool", lambda e: e.memset(state[:].rearrange("p a b -> p (a b)"), 0.0), writes=["state"])
            ph.op("pool", lambda e: e.memset(vconv[:].rearrange("p a b -> p (a b)"), 0.0), writes=["vconv"])
            ph.emit()

        def rstd_ops(ph, src_ap, dst_ap, n, keys_r, keys_w):
            ph.op("act", lambda e: e.activation(out=dst_ap, in_=src_ap, func=AF.Ln, scale=1.0 / n, bias=EPS),
                  reads=keys_r, writes=keys_w)
            ph.op("act", lambda e: e.activation(out=dst_ap, in_=dst_ap, func=AF.Exp, scale=-0.5),
                  reads=keys_w, writes=keys_w)

        def silu_ops(ph, z_ap, z_keys, g_ap, g_key):
            ph.op("act", lambda e: e.activation(out=g_ap, in_=z_ap, func=AF.Exp, scale=-1.0), reads=z_keys, writes=[g_key])
            ph.op("act", lambda e: e.activation(out=g_ap, in_=g_ap, func=AF.Ln, bias=1.0), reads=[g_key], writes=[g_key])
            ph.op("act", lambda e: e.activation(out=g_ap, in_=g_ap, func=AF.Exp, scale=-1.0), reads=[g_key], writes=[g_key])
            ph.op("dve", lambda e: e.tensor_tensor(out=g_ap, in0=z_ap, in1=g_ap, op=ALU.mult), reads=list(z_keys) + [g_key], writes=[g_key])

        def tile_program(l, ti, ph):
            meta = ti < 0
            T = N_META if meta else 512
            subs = [(0, N_META)] if meta else [(s * 128, 128) for s in range(4)]
            pos0 = 0 if meta else N_META + ti * 512
            first_layer = (l == 0)

            def hsrc(si, nr):
                if meta:
                    return (meta_d if first_layer else hmeta_d)[:, :]
                src = x_d if first_layer else out_d
                return src[ti * 512 + si * 128:ti * 512 + si * 128 + nr, :]

            if meta:
                ph.op("sp", lambda e: e.dma_start(out=rt_t[0:T, 0, :], in_=rt_d[0:T, :]), writes=["rt_t"], dma="rt_t")
            else:
                rv = rt_d[pos0:pos0 + 512, :].rearrange("(s p) d -> p s d", p=128)
                ph.op("sp", lambda e: e.dma_start(out=rt_t[:], in_=rv), writes=["rt_t"], dma="rt_t")
            for si, (r0, nr) in enumerate(subs):
                hb = si % 2
                hk = "hbuf%d" % hb
                ph.op("sp", lambda e: e.dma_start(out=hbuf[0:nr, hb, :], in_=hsrc(si, nr)), writes=[hk], dma=hk)
                uk = "u_bf%d" % hb
                ph.op("act", lambda e: e.activation(out=u_bf[0:nr, hb, :], in_=hbuf[0:nr, hb, :], func=AF.Square, accum_out=stats[0:nr, si, 0:1]),
                      reads=[hk], writes=[uk, "stats"])
                rstd_ops(ph, stats[0:nr, si, 0:1], stats[0:nr, si, 1:2], float(D_MODEL), ["stats"], ["stats"])
                ph.op("act", lambda e: e.activation(out=u_bf[0:nr, hb, :], in_=hbuf[0:nr, hb, :], func=AF.Copy, scale=stats[0:nr, si, 1:2]),
                      reads=[hk, "stats"], writes=[uk])
                for c in range(8):
                    ph.op("pe", lambda e: e.transpose(out=PT[:, c * 128:c * 128 + nr], in_=u_bf[0:nr, hb, c * 128:(c + 1) * 128], identity=ident[0:nr, 0:nr]),
                          reads=[uk, "ident"], writes=["PT"])
                ph.op("dve", lambda e: e.tensor_copy(out=uT[:, :, r0:r0 + nr], in_=PT[:, :].rearrange("p (c t) -> p c t", c=8)[:, :, 0:nr]),
                      reads=["PT"], writes=["uT"])

            if _DBG <= 1:
                return

            def fmajor(ft, bank):
                for c in range(8):
                    ph.op("pe", lambda e: e.matmul(PS[bank][:, 0:T], lhsT=win[:, c, ft * 128:(ft + 1) * 128], rhs=uT[:, c, 0:T],
                                                   start=(c == 0), stop=(c == 7)),
                          reads=["win", "uT"], writes=["PS%d" % bank])

            def conv_gen(a):
                cwb = SP_CW + a * 3
                fmajor(0 + a, 1)
                yield
                ph.op("act", lambda e: e.activation(out=cacc[:, 0:T], in_=PS[1][:, 0:T], func=AF.Copy), reads=["PS1"], writes=["cacc"])
                yield
                fmajor(4 + a, 1)
                yield
                ph.op("dve", lambda e: e.tensor_tensor(out=vconv[:, a, 2:2 + T], in0=PS[1][:, 0:T], in1=cacc[:, 0:T], op=ALU.mult),
                      reads=["PS1", "cacc"], writes=["vconv"])
                yield
                fmajor(2 + a, 1)
                ph.op("dve", lambda e: e.tensor_scalar(out=cacc[:, 0:T], in0=vconv[:, a, 0:T], scalar1=spt[:, l, cwb:cwb + 1],
                                                       scalar2=spt[:, l, SP_CB + a:SP_CB + a + 1], op0=ALU.mult, op1=ALU.add),
                      reads=["vconv", "spt"], writes=["cacc"])
                yield
                for j in (1, 2):
                    ph.op("dve", lambda e: e.scalar_tensor_tensor(out=cacc[:, 0:T], in0=vconv[:, a, j:j + T], scalar=spt[:, l, cwb + j:cwb + j + 1],
                                                                  in1=cacc[:, 0:T], op0=ALU.mult, op1=ALU.add),
                          reads=["vconv", "spt", "cacc"], writes=["cacc"])
                    yield
                ph.op("pool", lambda e: e.tensor_copy(out=vconv[:, a, 0:2], in_=vconv[:, a, T:T + 2]), reads=["vconv"], writes=["vconv"])
                ph.op("dve", lambda e: e.tensor_tensor(out=cacc[:, 0:T], in0=PS[1][:, 0:T], in1=cacc[:, 0:T], op=ALU.mult),
                      reads=["PS1", "cacc"], writes=["cacc"])
                yield
                fmajor(6 + a, 1)
                yield
                silu_ops(ph, PS[1][:, 0:T], ["PS1"], gate[:, 0:T], "gate")
                yield
                ph.op("pool", lambda e: e.tensor_tensor(out=yT[:, a, 0:T], in0=cacc[:, 0:T], in1=gate[:, 0:T], op=ALU.mult),
                      reads=["cacc", "gate"], writes=["yT%d" % a])
                yield

            if _DBG <= 2:
                return
            def proj_gen(sj):
                rj, nj = subs[sj]
                groups = [(0, 512, 2), (512, 512, 3), (1024, NTM - 1024, 4)]
                for (c0, cn, bank) in groups:
                    for c in range(8):
                        ph.op("pe", lambda e: e.matmul(PS[bank][0:nj, 0:cn], lhsT=uT[:, c, rj:rj + nj], rhs=win[:, c, NF + c0:NF + c0 + cn],
                                                       start=(c == 0), stop=(c == 7)),
                              reads=["win", "uT"], writes=["PS%d" % bank])
                        if c % 4 == 3:
                            yield
                ph.op("act", lambda e: e.activation(out=proj_t[0:nj, 0:512], in_=PS[2][0:nj, 0:512], func=AF.Copy), reads=["PS2"], writes=["proj_a"])
                yield
                ph.op("dve", lambda e: e.tensor_copy(out=proj_t[0:nj, 512:1024], in_=PS[3][0:nj, 0:512]), reads=["PS3"], writes=["proj_b"])
                yield
                ph.op("act", lambda e: e.activation(out=proj_t[0:nj, 1024:NTM], in_=PS[4][0:nj, 0:NTM - 1024], func=AF.Copy), reads=["PS4"], writes=["proj_c"])
                yield

            for si, (r0, nr) in enumerate(subs):
                if si == 0:
                    for _ in proj_gen(0):
                        pass
                if _DBG <= 2.1:
                    continue
                ph.op("act", lambda e: e.activation(out=junk[0:nr, 0:256], in_=proj_t[0:nr, 0:256], func=AF.Square, accum_out=stats[0:nr, si, 2:3]),
                      reads=["proj_a"], writes=["junk", "stats"])
                ph.op("act", lambda e: e.activation(out=junk[0:nr, 0:128], in_=proj_t[0:nr, 256:384], func=AF.Square, accum_out=stats[0:nr, si, 3:4]),
                      reads=["proj_a"], writes=["junk", "stats"])
                rstd_ops(ph, stats[0:nr, si, 2:3], stats[0:nr, si, 2:3], 256.0, ["stats"], ["stats"])
                rstd_ops(ph, stats[0:nr, si, 3:4], stats[0:nr, si, 3:4], 128.0, ["stats"], ["stats"])
                ph.op("dve", lambda e: e.tensor_scalar(out=cn_bf[0:nr, 0:256], in0=proj_t[0:nr, 0:256], scalar1=stats[0:nr, si, 2:3], scalar2=None, op0=ALU.mult),
                      reads=["proj_a", "stats"], writes=["cn_bf"])
                ph.op("dve", lambda e: e.tensor_scalar(out=cn_bf[0:nr, 256:384], in0=proj_t[0:nr, 256:384], scalar1=stats[0:nr, si, 3:4], scalar2=None, op0=ALU.mult),
                      reads=["proj_a", "stats"], writes=["cn_bf"])
                for j in range(3):
                    ph.op("pe", lambda e: e.transpose(out=PT[:, j * 128:j * 128 + nr], in_=cn_bf[0:nr, j * 128:(j + 1) * 128], identity=ident[0:nr, 0:nr]),
                          reads=["cn_bf", "ident"], writes=["PT"])
                ph.op("dve", lambda e: e.tensor_copy(out=cT[:, :, 0:nr], in_=PT[:, 0:384].rearrange("p (j t) -> p j t", j=3)[:, :, 0:nr]),
                      reads=["PT"], writes=["cT"])
                for half in range(2):
                    for c in range(2):
                        ph.op("pe", lambda e: e.matmul(PS[half][0:nr, 0:384], lhsT=cT[:, c, 0:nr], rhs=wuq[:, c, half * 384:(half + 1) * 384],
                                                       start=(c == 0), stop=(c == 1)),
                              reads=["cT", "wuq"], writes=["PS%d" % half])
                for half in range(2):
                    ph.op("pe", lambda e: e.matmul(PS[2 + half][0:nr, 0:512], lhsT=cT[:, 2, 0:nr], rhs=wukv[:, half * 512:(half + 1) * 512],
                                                   start=True, stop=True),
                          reads=["cT", "wukv"], writes=["PS%d" % (2 + half)])
                if _DBG <= 2.2:
                    continue
                ph.op("act", lambda e: e.activation(out=qw[0:nr, 0:4, :], in_=PS[0][0:nr, 0:384].rearrange("p (h d) -> p h d", h=4), func=AF.Copy),
                      reads=["PS0"], writes=["qw"])
                ph.op("dve", lambda e: e.tensor_copy(out=qw[0:nr, 4:8, :], in_=PS[1][0:nr, 0:384].rearrange("p (h d) -> p h d", h=4)),
                      reads=["PS1"], writes=["qw"])
                if _DBG <= 2.21:
                    continue
                for half in range(2):
                    kvv = PS[2 + half][0:nr, 0:512].rearrange("p (h c) -> p h c", h=4)
                    if half == 0:
                        ph.op("dve", lambda e: e.tensor_copy(out=kw[0:nr, 0:4, 0:64], in_=kvv[:, :, 0:64]), reads=["PS2"], writes=["kw"])
                    else:
                        ph.op("dve", lambda e: e.tensor_copy(out=kw[0:nr, 4:8, 0:64], in_=kvv[:, :, 0:64]), reads=["PS3"], writes=["kw"])
                    if _DBG <= 2.22:
                        continue
                    vdst = (v_meta[0:nr, :, :] if meta else v_cur[0:nr, si, :, :])
                    for hh in range(4):
                        hd = half * 4 + hh
                        cdst = 0 if hd % 2 == 0 else 64
                        ph.op("dve" if hh % 2 == 0 else "act",
                              (lambda e: e.tensor_copy(out=vdst[:, hd, cdst:cdst + 64], in_=kvv[:, hh, 64:128])) if hh % 2 == 0 else
                              (lambda e: e.activation(out=vdst[:, hd, cdst:cdst + 64], in_=kvv[:, hh, 64:128], func=AF.Copy)),
                              reads=["PS%d" % (2 + half)], writes=["vnew"])
                ph.op("dve", lambda e: e.tensor_copy(out=kw[0:nr, :, 64:96], in_=proj_t[0:nr, 384:416].unsqueeze(1).to_broadcast([nr, 8, 32])),
                      reads=["proj_a"], writes=["kw"])
                if _DBG <= 2.3:
                    continue
                def qk_chain(which):
                    wt = qw if which == 0 else kw
                    gain_ap = wq_s[:, :] if which == 0 else spt[:, l, SP_WK:SP_WK + 96]
                    wk = "qw" if which == 0 else "kw"
                    sk = "st2_%d" % which
                    sqk_ = "sqw%d" % which
                    sap = st2[0:nr, which * 8:which * 8 + 8]
                    sqb = sqw if which == 0 else sqk
                    sq3 = sqb[0:nr, 0:768].rearrange("p (h d) -> p h d", h=8)
                    ph.op("act", lambda e: e.activation(out=sqb[0:nr, 0:768], in_=wt[0:nr].rearrange("p h d -> p (h d)"), func=AF.Square), reads=[wk], writes=[sqk_])
                    yield
                    ph.op("dve", lambda e: e.tensor_reduce(out=sap, in_=sq3, axis=AX.X, op=ALU.add), reads=[sqk_], writes=[sk])
                    yield
                    rstd_ops(ph, sap, sap, 96.0, [sk], [sk])
                    yield
                    ph.op("dve", lambda e: e.tensor_tensor(out=wt[0:nr], in0=wt[0:nr], in1=sap.unsqueeze(2).to_broadcast([nr, 8, 96]), op=ALU.mult),
                          reads=[wk, sk], writes=[wk])
                    yield
                    ph.op("pool", lambda e: e.tensor_tensor(out=wt[0:nr], in0=wt[0:nr], in1=gain_ap[0:nr].unsqueeze(1).to_broadcast([nr, 8, 96]), op=ALU.mult),
                          reads=[wk, "wq_s", "spt"], writes=[wk])
                    yield
                    cosb = rt_t[0:nr, si, 0:16].unsqueeze(1).to_broadcast([nr, 8, 16])
                    sinb = rt_t[0:nr, si, 16:32].unsqueeze(1).to_broadcast([nr, 8, 16])
                    x1 = wt[0:nr, :, 64:80]
                    x2 = wt[0:nr, :, 80:96]
                    rk_ = "rw%d_" % which
                    ph.op("pool", lambda e: e.tensor_tensor(out=rw[0:nr, which, 0], in0=x1, in1=cosb, op=ALU.mult), reads=[wk, "rt_t"], writes=[rk_ + "0"])
                    ph.op("dve", lambda e: e.tensor_tensor(out=rw[0:nr, which, 2], in0=x1, in1=sinb, op=ALU.mult), reads=[wk, "rt_t"], writes=[rk_ + "2"])
                    yield
                    ph.op("pool", lambda e: e.tensor_tensor(out=rw[0:nr, which, 1], in0=x2, in1=sinb, op=ALU.mult), reads=[wk, "rt_t"], writes=[rk_ + "1"])
                    ph.op("dve", lambda e: e.tensor_tensor(out=rw[0:nr, which, 3], in0=x2, in1=cosb, op=ALU.mult), reads=[wk, "rt_t"], writes=[rk_ + "3"])
                    yield
                    ph.op("pool", lambda e: e.tensor_tensor(out=x1, in0=rw[0:nr, which, 0], in1=rw[0:nr, which, 1], op=ALU.subtract),
                          reads=[rk_ + "0", rk_ + "1", wk], writes=[wk])
                    ph.op("dve", lambda e: e.tensor_tensor(out=x2, in0=rw[0:nr, which, 2], in1=rw[0:nr, which, 3], op=ALU.add),
                          reads=[rk_ + "2", rk_ + "3", wk], writes=[wk])
                    yield
                    ph.op("act", lambda e: e.activation(out=qk_bf[0:nr, which].rearrange("p h d -> p (h d)"), in_=wt[0:nr].rearrange("p h d -> p (h d)"), func=AF.Copy),
                          reads=[wk], writes=["qk_bf%d" % which])
                    yield
                    if _DBG <= 2.4:
                        return
                    for h in range(8):
                        ph.op("pe", lambda e: e.transpose(out=PT[0:96, h * 128:h * 128 + nr], in_=qk_bf[0:nr, which, h, :], identity=ident[0:nr, 0:nr]),
                              reads=["qk_bf%d" % which, "ident"], writes=["PT"])
                    if meta and which == 1:
                        dstT, dkey = kT_meta[:, :, 0:nr], "kT_meta"
                    else:
                        dstT, dkey = (QT_cur if which == 0 else kT_cur)[:, :, r0:r0 + nr], ("QT_cur" if which == 0 else "kT_cur")
                    srcT = PT[0:96, :].rearrange("p (h t) -> p h t", h=8)[:, :, 0:nr]
                    ph.op("dve", lambda e: e.tensor_copy(out=dstT, in_=srcT), reads=["PT"], writes=[dkey])
                    yield

                qk = proj_t[0:nr, 416:928].rearrange("p (h d) -> p h d", h=8)
                cosr = rt_t[0:nr, si, 32:64].unsqueeze(1).to_broadcast([nr, 8, 32])
                sinr = rt_t[0:nr, si, 64:96].unsqueeze(1).to_broadcast([nr, 8, 32])
                rtmp = rtb[0:nr, :].rearrange("p (a h d) -> p a h d", a=4, h=8)
                pk = ["proj_a", "proj_b", "rt_t"]
                if _DBG > 3:
                    ph.op("pool", lambda e: e.tensor_tensor(out=rtmp[:, 0], in0=qk[:, :, 0:32], in1=cosr, op=ALU.mult), reads=pk, writes=["rt0"])
                    ph.op("dve", lambda e: e.tensor_tensor(out=rtmp[:, 2], in0=qk[:, :, 0:32], in1=sinr, op=ALU.mult), reads=pk, writes=["rt2"])
                    ph.op("pool", lambda e: e.tensor_tensor(out=rtmp[:, 1], in0=qk[:, :, 32:64], in1=sinr, op=ALU.mult), reads=pk, writes=["rt1"])
                    ph.op("dve", lambda e: e.tensor_tensor(out=rtmp[:, 3], in0=qk[:, :, 32:64], in1=cosr, op=ALU.mult), reads=pk, writes=["rt3"])
                    ph.op("act", lambda e: e.activation(out=rv_bf[0:nr].rearrange("p h d -> p (h d)"), in_=proj_t[0:nr, 928:1184], func=AF.Copy),
                          reads=["proj_b", "proj_c"], writes=["rv_bf"])
                    silu_ops(ph, proj_t[0:nr, 1184:1440], ["proj_c"], rg[0:nr, :], "rg")

                def ret_chain():
                    ph.op("pool", lambda e: e.tensor_tensor(out=rqk[0:nr, :, 0:32], in0=rtmp[:, 0], in1=rtmp[:, 1], op=ALU.subtract), reads=["rt0", "rt1"], writes=["rqk"])
                    ph.op("dve", lambda e: e.tensor_tensor(out=rqk[0:nr, :, 32:64], in0=rtmp[:, 2], in1=rtmp[:, 3], op=ALU.add), reads=["rt2", "rt3", "rqk"], writes=["rqk"])
                    yield
                    ph.op("act", lambda e: e.activation(out=rqk_bf[0:nr].rearrange("p h d -> p (h d)"), in_=rqk[0:nr].rearrange("p h d -> p (h d)"), func=AF.Copy),
                          reads=["rqk"], writes=["rqk_bf"])
                    yield
                    for ch in range(1 if meta else 2):
                        kcol = RT_KVDM if meta else (RT_KVDA if ch == 0 else RT_KVDB)
                        ph.op("pool", lambda e: e.tensor_tensor(out=kdec[0:nr, ch], in0=rqk[0:nr, 4:8, :],
                                                                in1=rtab[0:nr, kcol:kcol + 4].unsqueeze(2).to_broadcast([nr, 4, 64]), op=ALU.mult),
                              reads=["rqk", "rtab"], writes=["kdec%d" % ch])
                        yield
                    for j in range(4):
                        ph.op("pe", lambda e: e.transpose(out=PT[:, j * 128:j * 128 + nr], in_=rqk_bf[0:nr, 2 * j:2 * j + 2, :].rearrange("p a b -> p (a b)"),
                                                          identity=ident[0:nr, 0:nr]),
                              reads=["rqk_bf", "ident"], writes=["PT"])
                    ph.op("dve", lambda e: e.tensor_copy(out=rT[:, :, 0:nr], in_=PT[:, 0:512].rearrange("p (j t) -> p j t", j=4)[:, :, 0:nr]), reads=["PT"], writes=["rT"])
                    yield
                    ph.op("pool", lambda e: e.tensor_copy(out=qm[0:64, 0:4:2, 0:nr], in_=rT[0:64, 0:2, 0:nr]), reads=["rT"], writes=["qm"])
                    ph.op("pool", lambda e: e.tensor_copy(out=qm[64:128, 1:4:2, 0:nr], in_=rT[64:128, 0:2, 0:nr]), reads=["rT", "qm"], writes=["qm"])
                    yield
                    for h in range(4):
                        ph.op("pe", lambda e: e.matmul(PS[5][0:nr, h * 128:h * 128 + nr], lhsT=rT[:, 2 + h // 2, 0:nr], rhs=qm[:, h, 0:nr], start=True, stop=True),
                              reads=["rT", "qm"], writes=["PS5"])
                    ph.op("dve", lambda e: e.tensor_tensor(out=SD[0:nr, :, 0:nr], in0=PS[5][0:nr, :].rearrange("p (h c) -> p h c", h=4)[:, :, 0:nr],
                                                           in1=rtab[0:nr, RT_D2:RT_D2 + 512].rearrange("p (h c) -> p h c", h=4)[:, :, 0:nr], op=ALU.mult),
                          reads=["PS5", "rtab"], writes=["SD"])
                    yield
                    if not meta:
                        for ch in range(2):
                            qcol = RT_QDA if ch == 0 else RT_QDB
                            ph.op("pool", lambda e: e.tensor_tensor(out=qdm[:, ch], in0=qm[:], in1=rtab[:, qcol:qcol + 512].rearrange("p (h c) -> p h c", h=4), op=ALU.mult),
                                  reads=["qm", "rtab"], writes=["qdm%d" % ch])
                            yield
                        ph.op("pool", lambda e: e.tensor_copy(out=st_bf[:, 0], in_=state[:]), reads=["state"], writes=["st_bf0"])
                        yield
                    nch = 1 if meta else 2
                    for ch in range(nch):
                        for pr in range(2):
                            ph.op("pe", lambda e: e.matmul(PS[6][:, (ch * 2 + pr) * 128:(ch * 2 + pr + 1) * 128],
                                                           lhsT=kdec[0:nr, ch, 2 * pr:2 * pr + 2, :].rearrange("p a b -> p (a b)"),
                                                           rhs=rv_bf[0:nr, 2 * pr:2 * pr + 2, :].rearrange("p a b -> p (a b)"), start=True, stop=True),
                                  reads=["kdec%d" % ch, "rv_bf"], writes=["PS6"])
                        ph.op("pool", lambda e: e.tensor_tensor(out=st_tmp[:], in0=state[:], in1=rtab[:, RT_G:RT_G + 128].rearrange("p (a c) -> p a c", a=2), op=ALU.mult),
                              reads=["state", "rtab"], writes=["st_tmp"])
                        yield
                        kvv6 = PS[6][:, ch * 256:ch * 256 + 256].rearrange("p (a c) -> p a c", a=2)
                        ph.op("dve", lambda e: e.tensor_tensor(out=state[0:64], in0=kvv6[0:64, :, 0:64], in1=st_tmp[0:64], op=ALU.add),
                              reads=["PS6", "st_tmp"], writes=["state"])
                        ph.op("dve", lambda e: e.tensor_tensor(out=state[64:128], in0=kvv6[64:128, :, 64:128], in1=st_tmp[64:128], op=ALU.add),
                              reads=["PS6", "st_tmp", "state"], writes=["state"])
                        yield
                        if ch == 0 and not meta:
                            ph.op("pool", lambda e: e.tensor_copy(out=st_bf[:, 1], in_=state[:]), reads=["state"], writes=["st_bf1"])
                            yield
                    for h in range(4):
                        ph.op("pe", lambda e: e.matmul(PS[0][0:nr, h * 64:(h + 1) * 64], lhsT=SD[0:nr, h, 0:nr], rhs=rv_bf[0:nr, h, :], start=True, stop=meta),
                              reads=["SD", "rv_bf"], writes=["PS0"])
                        if not meta:
                            for ch in range(2):
                                ph.op("pe", lambda e: e.matmul(PS[0][:, h * 64:(h + 1) * 64], lhsT=qdm[:, ch, h, :], rhs=st_bf[:, ch, h // 2, :],
                                                               start=False, stop=(ch == 1)),
                                      reads=["qdm0", "qdm1", "st_bf0", "st_bf1"], writes=["PS0"])
                    rs = st2[0:nr, 16:20]
                    ph.op("act", lambda e: e.activation(out=o_sb[0:nr].rearrange("p h d -> p (h d)"), in_=PS[0][0:nr, 0:256], func=AF.Copy), reads=["PS0"], writes=["o_sb"])
                    yield
                    ph.op("act", lambda e: e.activation(out=o_sq[0:nr].rearrange("p h d -> p (h d)"), in_=o_sb[0:nr].rearrange("p h d -> p (h d)"), func=AF.Square),
                          reads=["o_sb"], writes=["o_sq"])
                    yield
                    ph.op("dve", lambda e: e.tensor_reduce(out=rs, in_=o_sq[0:nr], axis=AX.X, op=ALU.add), reads=["o_sq"], writes=["st2_r"])
                    yield
                    rstd_ops(ph, rs, rs, 64.0, ["st2_r"], ["st2_r"])
                    yield
                    ph.op("dve", lambda e: e.tensor_tensor(out=o_sb[0:nr], in0=o_sb[0:nr], in1=rs.unsqueeze(2).to_broadcast([nr, 4, 64]), op=ALU.mult),
                          reads=["o_sb", "st2_r"], writes=["o_sb"])
                    yield
                    ph.op("pool", lambda e: e.tensor_tensor(out=o_sb[0:nr], in0=o_sb[0:nr], in1=spt[0:nr, l, SP_WR:SP_WR + 64].unsqueeze(1).to_broadcast([nr, 4, 64]), op=ALU.mult),
                          reads=["o_sb", "spt"], writes=["o_sb"])
                    yield
                    ph.op("pool", lambda e: e.tensor_tensor(out=yr_bf[0:nr, :], in0=o_sb[0:nr].rearrange("p h d -> p (h d)"), in1=rg[0:nr, :], op=ALU.mult),
                          reads=["o_sb", "rg"], writes=["yr_bf"])
                    yield
                    for j in range(2):
                        ph.op("pe", lambda e: e.transpose(out=PT[:, j * 128:j * 128 + nr], in_=yr_bf[0:nr, j * 128:(j + 1) * 128], identity=ident[0:nr, 0:nr]),
                              reads=["yr_bf", "ident"], writes=["PT"])
                    ph.op("dve", lambda e: e.tensor_copy(out=yT[:, 6:8, r0:r0 + nr], in_=PT[:, 0:256].rearrange("p (j t) -> p j t", j=2)[:, :, 0:nr]), reads=["PT"], writes=["yT_r"])
                    yield

                gens = [qk_chain(0), qk_chain(1)] + ([ret_chain()] if _DBG > 3 else [])
                if si + 1 < len(subs):
                    gens.append(proj_gen(si + 1))
                if meta:
                    def conv_both():
                        yield from conv_gen(0)
                        yield from conv_gen(1)
                    gens.append(conv_both())
                elif si < 2:
                    gens.append(conv_gen(si))
                while gens:
                    for g in list(gens):
                        try:
                            next(g)
                        except StopIteration:
                            gens.remove(g)
            if _DBG <= 4:
                return
            if not meta:
                ph.op("sp", lambda e: e.dma_start(out=kc_d[:, :, ti * 512:(ti + 1) * 512].rearrange("h d t -> d h t"), in_=kT_cur[:]),
                      reads=["kT_cur"], writes=["kc_dram"], dma="kc_st")
                for h in range(8):
                    ph.op("sp", lambda e: e.dma_start(out=vc_d[h, :, ti * 4:(ti + 1) * 4, :], in_=v_cur[:, :, h, :]),
                          reads=["vnew"], writes=["vc_dram"], dma="vc_st")
            descs = []
            loads = []
            nload = 0
            for h in range(8):
                ob = 5 + (h % 2)
                hd = []
                if meta:
                    hd.append(dict(kT=kT_meta[:, h, 0:T], kk=["kT_meta"], v=v_meta[0:T, h, :], vk=["vnew"], nk=T, q0=0, diag=False))
                else:
                    hd.append(dict(kT=kT_meta[:, h, :], kk=["kT_meta"], v=v_meta[:, h, :], vk=["v_meta"], nk=N_META, q0=0, diag=False))
                    npast = ti * 512
                    for k0 in range(0, npast, KCH):
                        kn = min(KCH, npast - k0)
                        bi = nload % NKB
                        nload += 1

                        def ld(h=h, bi=bi, k0=k0, kn=kn):
                            ph.op("sp", lambda e: e.dma_start(out=kbuf[bi][:, 0:kn], in_=kc_d[h, :, k0:k0 + kn]),
                                  reads=["kc_dram"], writes=["kbuf%d" % bi], dma="kbuf%d" % bi)
                            ph.op("sp", lambda e: e.dma_start(out=vbuf[bi][:, 0:kn // 128, :], in_=vc_d[h, :, k0 // 128:(k0 + kn) // 128, :]),
                                  reads=["vc_dram"], writes=["vbuf%d" % bi], dma="vbuf%d" % bi)
                        loads.append((len(descs) + len(hd), ld))
                        for j in range(kn // 128):
                            hd.append(dict(kT=kbuf[bi][:, j * 128:(j + 1) * 128], kk=["kbuf%d" % bi], v=vbuf[bi][:, j, :], vk=["vbuf%d" % bi],
                                           nk=128, q0=0, diag=False))
                    for r in range(4):
                        hd.append(dict(kT=kT_cur[:, h, r * 128:(r + 1) * 128], kk=["kT_cur"], v=v_cur[:, r, h, :], vk=["vnew"], nk=128, q0=r * 128, diag=True))
                for i, d in enumerate(hd):
                    d.update(h=h, ob=ob, first=(i == 0), last=(i == len(hd) - 1))
                descs += hd

            SBK = (0, 1, 2, 4)

            def emit_qk(i):
                d = descs[i]
                h, sbk, nq = d["h"], SBK[i % 4], T - d["q0"]
                kT_ap, nk, q0 = d["kT"], d["nk"], d["q0"]
                ph.op("pe", lambda e: e.matmul(PS[sbk][0:nk, 0:nq], lhsT=kT_ap, rhs=QT_cur[:, h, q0:T], start=True, stop=True),
                      reads=list(d["kk"]) + ["QT_cur"], writes=["PS%d" % sbk])

            def emit_pv(i):
                d = descs[i]
                h, ob, sbk, pb, nq = d["h"], d["ob"], SBK[i % 4], i % NPB, T - d["q0"]
                okey = "PS%d" % ob
                nk, q0, v_ap = d["nk"], d["q0"], d["v"]
                st_flag, last = d["first"], d["last"]
                ph.op("act", lambda e: e.activation(out=pbuf[pb][0:nk, 0:nq], in_=PS[sbk][0:nk, 0:nq], func=AF.Exp), reads=["PS%d" % sbk], writes=["pbuf%d" % pb])
                if d["diag"]:
                    ph.op("pool", lambda e: e.memset(pbuf[pb][64:128, 0:64], 0.0), reads=["pbuf%d" % pb], writes=["pbuf%d" % pb])
                ph.op("pe", lambda e: e.matmul(PS[ob][:, q0:T], lhsT=v_ap, rhs=pbuf[pb][0:nk, 0:nq], start=st_flag, stop=last),
                      reads=list(d["vk"]) + ["pbuf%d" % pb], writes=[okey])
                if last:
                    orow = slice(0, 64) if h % 2 == 0 else slice(64, 128)
                    srow = slice(64, 128) if h % 2 == 0 else slice(0, 64)
                    if ti < 4:
                        ph.op("act", lambda e: e.activation(out=recip[srow, 0:T], in_=PS[ob][srow, 0:T], func=AF.Ln), reads=[okey], writes=["recip%d" % (h % 2)])
                        ph.op("act", lambda e: e.activation(out=recip[srow, 0:T], in_=recip[srow, 0:T], func=AF.Exp, scale=-1.0),
                              reads=["recip%d" % (h % 2)], writes=["recip%d" % (h % 2)])
                    else:
                        ph.op("dve", lambda e: e.reciprocal(out=recip[srow, 0:T], in_=PS[ob][srow, 0:T]), reads=[okey], writes=["recip%d" % (h % 2)])
                    ph.op("dve", lambda e: e.tensor_tensor(out=on_sb[orow, 0:T], in0=PS[ob][orow, 0:T], in1=recip[srow, 0:T], op=ALU.mult),
                          reads=[okey, "recip%d" % (h % 2)], writes=["on_sb%d" % (h % 2)])
                    if h % 2 == 1:
                        gb, gk = GB[(h // 2) % 2]
                        ph.op("pool", lambda e: e.tensor_tensor(out=yT[:, 2 + h // 2, 0:T], in0=on_sb[:, 0:T], in1=gb[:, 0:T], op=ALU.mult),
                              reads=["on_sb0", "on_sb1", gk], writes=["yT_m%d" % (h // 2)])
                        if h // 2 + 2 < 4:
                            emit_gate(h // 2 + 2)

            GB = ((gate, "gate"), (cacc, "cacc"))

            def emit_gate(a):
                gb, gk = GB[a % 2]
                fmajor(8 + a, 3)
                silu_ops(ph, PS[3][:, 0:T], ["PS3"], gb[:, 0:T], gk)

            emit_gate(0)
            emit_gate(1)
            LA, LL = 3, 12
            nd = len(descs)
            li = 0
            for i in range(nd + LA):
                while li < len(loads) and loads[li][0] <= i + LL:
                    loads[li][1]()
                    li += 1
                if i < nd:
                    emit_qk(i)
                if i - LA >= 0:
                    emit_pv(i - LA)
            if _DBG <= 5:
                return
            ykeys = ["yT0", "yT1", "yT_r"] + ["yT_m%d" % a for a in range(4)]
            for si, (r0, nr) in enumerate(subs):
                hb = si % 2
                hk = "hbuf%d" % hb
                ph.op("sp", lambda e: e.dma_start(out=hbuf[0:nr, hb, :], in_=hsrc(si, nr)), writes=[hk], dma=hk)
                for half in range(2):
                    bank = 3 + half
                    for c in range(8):
                        ph.op("pe", lambda e: e.matmul(PS[bank][0:nr, :], lhsT=yT[:, c, r0:r0 + nr], rhs=wout[:, c, half * 512:(half + 1) * 512],
                                                       start=(c == 0), stop=(c == 7)),
                              reads=ykeys + ["wout"], writes=["PS%d" % bank])
                    ph.op("dve", lambda e: e.tensor_tensor(out=hbuf[0:nr, hb, half * 512:(half + 1) * 512], in0=PS[bank][0:nr, :],
                                                           in1=hbuf[0:nr, hb, half * 512:(half + 1) * 512], op=ALU.add),
                          reads=["PS%d" % bank, hk], writes=[hk])
                if meta:
                    ph.op("sp", lambda e: e.dma_start(out=hmeta_d[:, :], in_=hbuf[0:nr, hb, :]), reads=[hk], dma="h_st%d" % hb)
                else:
                    ph.op("sp", lambda e: e.dma_start(out=out_d[ti * 512 + si * 128:ti * 512 + si * 128 + nr, :], in_=hbuf[0:nr, hb, :]),
                          reads=[hk], dma="h_st%d" % hb)

        for l in range(DEPTH):
            load_weights(l)
            if _DBG <= 0:
                continue
            tiles = list(range(-1, NT))
            for g0 in range(0, len(tiles), TILES_PER_BLOCK):
                ph = Phase(nc)
                for ti in tiles[g0:g0 + TILES_PER_BLOCK]:
                    tile_program(l, ti, ph)
                ph.emit()
    return nc


_CACHE = {}


def _prep_inputs(x, meta_tokens, ln_w, w_in, w_out, conv_w, conv_b, q_a_norm, w_uq, kv_a_norm, w_ukv,
                 q_norm, k_norm, ret_norm):
    DEPTH = w_in.shape[0]
    SEQ = x.shape[1]
    f = np.float32
    sp = np.zeros((128, DEPTH, NSP), f)
    for l in range(DEPTH):
        sp[:, l, SP_LNW:SP_LNW + 8] = np.asarray(ln_w[l], f).reshape(8, 128).T
        sp[:, l, SP_QAN:SP_QAN + 2] = np.asarray(q_a_norm[l], f).reshape(2, 128).T
        sp[:, l, SP_KVAN] = np.asarray(kv_a_norm[l], f)
        cw = np.asarray(conv_w[l], f)
        for a in range(2):
            sp[:, l, SP_CW + a * 3:SP_CW + a * 3 + 3] = cw[:, a * 128:(a + 1) * 128].T
            sp[:, l, SP_CB + a] = np.asarray(conv_b[l], f)[a * 128:(a + 1) * 128]
        sp[:, l, SP_WQ:SP_WQ + 96] = np.asarray(q_norm[l], f)[None, :]
        sp[:, l, SP_WK:SP_WK + 96] = np.asarray(k_norm[l], f)[None, :]
        sp[:, l, SP_WR:SP_WR + 64] = np.asarray(ret_norm[l], f)[None, :]
    common = {
        "meta": np.ascontiguousarray(np.asarray(meta_tokens, f)),
        "w_in": np.ascontiguousarray(np.asarray(w_in, f)[:, :, _PERM]),
        "w_out": np.ascontiguousarray(np.asarray(w_out, f)),
        "w_uq": np.ascontiguousarray(np.asarray(w_uq, f)),
        "w_ukv": np.ascontiguousarray(np.asarray(w_ukv, f)),
        "sp": sp,
        "rt": _rope_table(N_META + SEQ),
        "rtab": _ret_tables(),
        "ident": np.eye(128, dtype=f),
    }
    return common


def kernel(x, meta_tokens, ln_w, w_in, w_out, conv_w, conv_b, q_a_norm, w_uq, kv_a_norm, w_ukv,
           q_norm, k_norm, ret_norm):
    x = np.asarray(x, np.float32)
    B, SEQ, _ = x.shape
    DEPTH = np.asarray(w_in).shape[0]
    key = (SEQ, DEPTH)
    if key not in _CACHE:
        _CACHE[key] = build_program(SEQ, DEPTH)
    nc = _CACHE[key]
    common = _prep_inputs(x, meta_tokens, ln_w, w_in, w_out, conv_w, conv_b, q_a_norm, w_uq, kv_a_norm, w_ukv,
                          q_norm, k_norm, ret_norm)
    in_maps = []
    for b in range(B):
        m = dict(common)
        m["x"] = np.ascontiguousarray(x[b])
        in_maps.append(m)
    res = run_bass_kernel_spmd(nc, in_maps, core_ids=list(range(B)))
    return np.stack([np.asarray(r["out"], np.float32) for r in res.results], axis=0)
```
